# Optimizing a Trainium2 kernel written in Bass

```python
import math
import jax, jax.numpy as jnp
from jax import lax
import numpy as np

D_MODEL = 2048
BATCH = 4
SEQ = 4096
DEPTH = 1

N_MEM = 256
NORM_EPS = 1e-6

SSD_WIDTH = D_MODEL // 2
SSD_HEAD_DIM = 64
SSD_HEADS = SSD_WIDTH // SSD_HEAD_DIM
SSD_GROUPS = 2
SSD_HEADS_PER_GROUP = SSD_HEADS // SSD_GROUPS
SSD_STATE = 128
SSD_CONV = 4
SSD_CHUNK = 128
SSD_CONV_DIM = SSD_WIDTH + 2 * SSD_GROUPS * SSD_STATE
SSD_IN = SSD_WIDTH + SSD_CONV_DIM + SSD_HEADS

RWKV_WIDTH = D_MODEL - SSD_WIDTH
RWKV_HEAD_DIM = 64
RWKV_HEADS = RWKV_WIDTH // RWKV_HEAD_DIM
RWKV_DECAY_RANK = 96
RWKV_AAA_RANK = 96
RWKV_GATE_RANK = 256
RWKV_IN = 3 * RWKV_WIDTH + RWKV_DECAY_RANK + RWKV_AAA_RANK + RWKV_GATE_RANK
RWKV_LN_EPS = 64e-5

MIX_WIDTH = SSD_WIDTH + RWKV_WIDTH
D_IN = SSD_IN + RWKV_IN

XATTN_HEADS = 4
XATTN_HEAD_DIM = D_MODEL // XATTN_HEADS

D_FF = 4 * D_MODEL

kernel_name = "hymba_ssd_rwkv7_memxattn_block"


def rms_norm(x, g, eps=NORM_EPS):
    xf = x.astype(jnp.float32)
    y = xf * lax.rsqrt(jnp.mean(xf * xf, axis=-1, keepdims=True) + eps)
    return (y * g).astype(x.dtype)


def causal_depthwise_conv(u, w, b):
    y = lax.conv_general_dilated(
        u, w, window_strides=(1,), padding=[(w.shape[0] - 1, 0)],
        dimension_numbers=("NWC", "WIO", "NWC"), feature_group_count=u.shape[-1])
    return y + b


def ssd_mixer(u, conv_w, conv_b, dt_bias, a_log, d_skip, norm_g):
    f32 = jnp.float32
    bsz, seq, _ = u.shape
    G, E, P, N, Q = SSD_GROUPS, SSD_HEADS_PER_GROUP, SSD_HEAD_DIM, SSD_STATE, SSD_CHUNK
    nc = seq // Q
    z, xbc, dt = jnp.split(u, [SSD_WIDTH, SSD_WIDTH + SSD_CONV_DIM], axis=-1)
    xbc = jax.nn.silu(causal_depthwise_conv(xbc, conv_w, conv_b))
    xs, bm, cm = jnp.split(xbc, [SSD_WIDTH, SSD_WIDTH + G * N], axis=-1)
    xs = xs.astype(f32).reshape(bsz, nc, Q, G, E, P)
    bm = bm.astype(f32).reshape(bsz, nc, Q, G, N)
    cm = cm.astype(f32).reshape(bsz, nc, Q, G, N)
    dt = jax.nn.softplus(dt.astype(f32) + dt_bias.astype(f32))
    a = -jnp.exp(a_log.astype(f32))
    dt_c = dt.reshape(bsz, nc, Q, G, E)
    xdt = xs * dt_c[..., None]
    da = jnp.transpose(dt_c * a.reshape(G, E), (0, 1, 3, 4, 2))
    cs = jnp.cumsum(da, axis=-1)
    causal = jnp.tril(jnp.ones((Q, Q), dtype=bool))
    seg = cs[..., :, None] - cs[..., None, :]
    lmat = jnp.where(causal, jnp.exp(jnp.where(causal, seg, 0.0)), 0.0)
    cb = jnp.einsum('bclgn,bcsgn->bcgls', cm, bm)
    y_diag = jnp.einsum('bcgls,bcgels,bcsgep->bclgep', cb, lmat, xdt)
    decay_to_end = jnp.exp(cs[..., -1:] - cs)
    chunk_states = jnp.einsum('bcsgn,bcges,bcsgep->bcgepn', bm, decay_to_end, xdt)
    chunk_decay = jnp.exp(cs[..., -1])

    def carry_state(h, inp):
        st, dec = inp
        return h * dec[..., None, None] + st, h

    h0 = jnp.zeros((bsz, G, E, P, N), f32)
    _, start_states = lax.scan(carry_state, h0,
                               (jnp.moveaxis(chunk_states, 1, 0), jnp.moveaxis(chunk_decay, 1, 0)))
    start_states = jnp.moveaxis(start_states, 0, 1)
    y_off = jnp.einsum('bclgn,bcgepn,bcgel->bclgep', cm, start_states, jnp.exp(cs))
    y = y_diag + y_off + xs * d_skip.astype(f32).reshape(G, E, 1)
    y = y.reshape(bsz, seq, SSD_WIDTH) * jax.nn.silu(z.astype(f32))
    yg = y.reshape(bsz, seq, G, SSD_WIDTH // G)
    yg = yg * lax.rsqrt(jnp.mean(yg * yg, axis=-1, keepdims=True) + NORM_EPS)
    y = yg.reshape(bsz, seq, SSD_WIDTH) * norm_g
    return y.astype(u.dtype)


def rwkv7_mixer(u, mu, w0, w2, a0, a2, g2, k_k, k_a, r_k, ln_w, ln_b):
    f32 = jnp.float32
    bsz, seq, _ = u.shape
    H, N, W = RWKV_HEADS, RWKV_HEAD_DIM, RWKV_WIDTH
    uf = u.astype(f32)
    u_prev = jnp.pad(uf, ((0, 0), (1, 0), (0, 0)))[:, :-1]
    uf = uf + (u_prev - uf) * mu
    r, k, v, pw, pa, pg = jnp.split(
        uf, [W, 2 * W, 3 * W, 3 * W + RWKV_DECAY_RANK,
             3 * W + RWKV_DECAY_RANK + RWKV_AAA_RANK], axis=-1)
    w_log = -jax.nn.softplus(-(w0 + jnp.tanh(pw) @ w2)) - 0.5
    decay = jnp.exp(-jnp.exp(w_log))
    iclr = jax.nn.sigmoid(a0 + pa @ a2)
    gate = jax.nn.sigmoid(pg) @ g2
    heads = lambda t: t.reshape(bsz, seq, H, N)
    kk = heads(k * k_k)
    kk = kk / jnp.maximum(jnp.sqrt(jnp.sum(kk * kk, axis=-1, keepdims=True)), 1e-12)
    k = k * (1.0 + (iclr - 1.0) * k_a)
    r, k, v, decay, iclr = heads(r), heads(k), heads(v), heads(decay), heads(iclr)

    def step(state, inp):
        r_t, w_t, k_t, v_t, kk_t, a_t = inp
        sa = jnp.einsum('bhij,bhj->bhi', state, -kk_t)
        state = (state * w_t[:, :, None, :]
                 + sa[..., None] * (kk_t * a_t)[:, :, None, :]
                 + v_t[..., None] * k_t[:, :, None, :])
        return state, jnp.einsum('bhij,bhj->bhi', state, r_t)

    seq_first = lambda t: jnp.moveaxis(t, 1, 0)
    s0 = jnp.zeros((bsz, H, N, N), f32)
    _, y = lax.scan(step, s0, (seq_first(r), seq_first(decay), seq_first(k),
                               seq_first(v), seq_first(kk), seq_first(iclr)))
    y = jnp.moveaxis(y, 0, 1)
    mean = jnp.mean(y, axis=-1, keepdims=True)
    var = jnp.mean(jnp.square(y - mean), axis=-1, keepdims=True)
    y = ((y - mean) * lax.rsqrt(var + RWKV_LN_EPS)).reshape(bsz, seq, W) * ln_w + ln_b
    bonus = jnp.sum(r * k * r_k, axis=-1, keepdims=True) * v
    y = (y + bonus.reshape(bsz, seq, W)) * gate
    return y.astype(u.dtype)


def memory_cross_attention(h, m, wq, wk, wv, wo):
    bsz, seq, _ = h.shape
    q = (h @ wq).reshape(bsz, seq, XATTN_HEADS, XATTN_HEAD_DIM)
    k = (m @ wk).reshape(bsz, m.shape[1], XATTN_HEADS, XATTN_HEAD_DIM)
    v = (m @ wv).reshape(bsz, m.shape[1], XATTN_HEADS, XATTN_HEAD_DIM)
    scores = jnp.einsum('bshd,bmhd->bhsm', q, k).astype(jnp.float32) * (XATTN_HEAD_DIM ** -0.5)
    p = jax.nn.softmax(scores, axis=-1).astype(v.dtype)
    o = jnp.einsum('bhsm,bmhd->bshd', p, v).reshape(bsz, seq, D_MODEL)
    return o @ wo


def setup_inputs(seed: int = 0) -> dict:
    key = jax.random.key(seed)
    ks = iter(jax.random.split(key, 40))
    nrm = lambda shape, scale: jax.random.normal(next(ks), shape, jnp.float32) * scale
    uni = lambda shape, lo, hi: jax.random.uniform(next(ks), shape, jnp.float32, lo, hi)
    L = DEPTH
    dt0 = jnp.exp(uni((L, SSD_HEADS), math.log(1e-3), math.log(1e-1)))
    return {
        "x": nrm((BATCH, SEQ, D_MODEL), 1.0),
        "mem": nrm((BATCH, N_MEM, D_MODEL), 1.0),
        "norm_mix_g": 1.0 + nrm((L, D_MODEL), 0.02),
        "w_in": nrm((L, D_MODEL, D_IN), D_MODEL ** -0.5),
        "ssd_conv_w": nrm((L, SSD_CONV, 1, SSD_CONV_DIM), SSD_CONV ** -0.5),
        "ssd_conv_b": nrm((L, SSD_CONV_DIM), 0.01),
        "ssd_dt_bias": dt0 + jnp.log(-jnp.expm1(-dt0)),
        "ssd_a_log": jnp.log(uni((L, SSD_HEADS), 1.0, 16.0)),
        "ssd_d": 1.0 + nrm((L, SSD_HEADS), 0.1),
        "ssd_norm_g": 1.0 + nrm((L, SSD_WIDTH), 0.02),
        "rwkv_mu": uni((L, RWKV_IN), 0.0, 1.0),
        "rwkv_w0": uni((L, RWKV_WIDTH), -6.0, -1.0),
        "rwkv_w2": nrm((L, RWKV_DECAY_RANK, RWKV_WIDTH), 0.5 * RWKV_DECAY_RANK ** -0.5),
        "rwkv_a0": nrm((L, RWKV_WIDTH), 0.1),
        "rwkv_a2": nrm((L, RWKV_AAA_RANK, RWKV_WIDTH), RWKV_AAA_RANK ** -0.5),
        "rwkv_g2": nrm((L, RWKV_GATE_RANK, RWKV_WIDTH), RWKV_GATE_RANK ** -0.5),
        "rwkv_k_k": 0.85 + nrm((L, RWKV_WIDTH), 0.05),
        "rwkv_k_a": 1.0 + nrm((L, RWKV_WIDTH), 0.05),
        "rwkv_r_k": nrm((L, RWKV_HEADS, RWKV_HEAD_DIM), 0.1),
        "rwkv_ln_w": 1.0 + nrm((L, RWKV_WIDTH), 0.02),
        "rwkv_ln_b": nrm((L, RWKV_WIDTH), 0.01),
        "w_out": nrm((L, MIX_WIDTH, D_MODEL), MIX_WIDTH ** -0.5),
        "norm_x_g": 1.0 + nrm((L, D_MODEL), 0.02),
        "norm_mem_g": 1.0 + nrm((L, D_MODEL), 0.02),
        "xattn_wq": nrm((L, D_MODEL, D_MODEL), D_MODEL ** -0.5),
        "xattn_wk": nrm((L, D_MODEL, D_MODEL), D_MODEL ** -0.5),
        "xattn_wv": nrm((L, D_MODEL, D_MODEL), D_MODEL ** -0.5),
        "xattn_wo": nrm((L, D_MODEL, D_MODEL), D_MODEL ** -0.5),
        "norm_ffn_g": 1.0 + nrm((L, D_MODEL), 0.02),
        "ffn_w1": nrm((L, D_MODEL, D_FF), D_MODEL ** -0.5),
        "ffn_w2": nrm((L, D_FF, D_MODEL), D_FF ** -0.5),
        "final_norm_g": 1.0 + nrm((D_MODEL,), 0.02),
    }


def reference(x, mem, norm_mix_g, w_in, ssd_conv_w, ssd_conv_b, ssd_dt_bias, ssd_a_log,
              ssd_d, ssd_norm_g, rwkv_mu, rwkv_w0, rwkv_w2, rwkv_a0, rwkv_a2, rwkv_g2,
              rwkv_k_k, rwkv_k_a, rwkv_r_k, rwkv_ln_w, rwkv_ln_b, w_out, norm_x_g,
              norm_mem_g, xattn_wq, xattn_wk, xattn_wv, xattn_wo, norm_ffn_g, ffn_w1,
              ffn_w2, final_norm_g):
    for l in range(DEPTH):
        h = rms_norm(x, norm_mix_g[l])
        u = h @ w_in[l]
        y_ssd = ssd_mixer(u[..., :SSD_IN], ssd_conv_w[l], ssd_conv_b[l], ssd_dt_bias[l],
                          ssd_a_log[l], ssd_d[l], ssd_norm_g[l])
        y_rwkv = rwkv7_mixer(u[..., SSD_IN:], rwkv_mu[l], rwkv_w0[l], rwkv_w2[l], rwkv_a0[l],
                             rwkv_a2[l], rwkv_g2[l], rwkv_k_k[l], rwkv_k_a[l], rwkv_r_k[l],
                             rwkv_ln_w[l], rwkv_ln_b[l])
        x = x + jnp.concatenate([y_ssd, y_rwkv], axis=-1) @ w_out[l]
        h = rms_norm(x, norm_x_g[l])
        m = rms_norm(mem, norm_mem_g[l])
        x = x + memory_cross_attention(h, m, xattn_wq[l], xattn_wk[l], xattn_wv[l], xattn_wo[l])
        h = rms_norm(x, norm_ffn_g[l])
        x = x + jnp.square(jax.nn.relu(h @ ffn_w1[l])) @ ffn_w2[l]
    return rms_norm(x, final_norm_g)
```

```python
import numpy as np
import concourse.bass as bass
import concourse.mybir as mybir
from concourse.bass_utils import run_bass_kernel_spmd

F32 = mybir.dt.float32
BF16 = mybir.dt.bfloat16
AF = mybir.ActivationFunctionType
ALU = mybir.AluOpType
AX = mybir.AxisListType

ENGS = ["pe", "act", "dve", "pool", "sp"]


class V:
    __slots__ = ("ap", "keys")

    def __init__(self, ap, keys):
        self.ap = ap
        self.keys = [keys] if isinstance(keys, tuple) else keys


class Op:
    __slots__ = ("eng", "idx", "fn", "deps", "needed", "signum", "isdma", "slot", "dval", "epoch")

    def __init__(self, eng, idx, fn, isdma):
        self.eng, self.idx, self.fn, self.isdma = eng, idx, fn, isdma
        self.epoch = 0
        self.deps = {}
        self.needed = False
        self.signum = 0
        self.slot = None
        self.dval = 0


class St:
    __slots__ = ("writer", "readers")

    def __init__(self):
        self.writer = None
        self.readers = []


class Sched:
    def __init__(self):
        self.ops = {e: [] for e in ENGS}
        self.state = {}
        self.ndma = {"pool": 4, "sp": 8}
        self.dma_count = {q: 0 for q in self.ndma}
        self.epoch = 0

    def new_epoch(self):
        self.epoch += 1

    def _dep(self, op, d):
        if d is None or d is op:
            return
        if d.eng == "pe" and op.eng == "pe" and not d.isdma and not op.isdma:
            return
        k = ("dma", d.eng, d.slot) if d.isdma else d.eng
        cur = op.deps.get(k)
        if cur is None or (d.dval if d.isdma else d.idx) > (cur.dval if cur.isdma else cur.idx):
            op.deps[k] = d

    def _touch(self, op, key, is_write):
        name, sub = key
        ent = self.state.setdefault(name, {})
        if sub is None:
            targets = list(ent.keys())
        else:
            targets = [k for k in (sub, None) if k in ent]
        for k in targets:
            st = ent[k]
            self._dep(op, st.writer)
            if is_write:
                for r in st.readers:
                    self._dep(op, r)
        if sub is None and is_write:
            ent.clear()
        st = ent.get(sub)
        if st is None:
            st = ent[sub] = St()
        if is_write:
            st.writer = op
            st.readers = []
        else:
            st.readers.append(op)

    def add(self, eng, fn, r=(), w=(), dma=False):
        lst = self.ops[eng]
        op = Op(eng, len(lst), fn, dma)
        op.epoch = self.epoch
        if dma:
            n = self.dma_count[eng]
            self.dma_count[eng] = n + 1
            op.slot = n % self.ndma[eng]
            op.dval = 16 * (n // self.ndma[eng] + 1)
        for v in r:
            for k in (v.keys if isinstance(v, V) else [v]):
                self._touch(op, k, False)
        for v in w:
            for k in (v.keys if isinstance(v, V) else [v]):
                self._touch(op, k, True)
        lst.append(op)
        return op

    def finalize(self):
        CE = ["pe", "act", "dve", "pool"]
        for e in ENGS:
            for op in self.ops[e]:
                for k, d in list(op.deps.items()):
                    if (not d.isdma) and d.epoch != op.epoch:
                        del op.deps[k]
                    else:
                        d.needed = True
        self.final = [dict() for _ in range(self.epoch + 1)]
        for e in CE:
            last = {}
            for op in self.ops[e]:
                if not op.isdma:
                    last[op.epoch] = op
            for op in last.values():
                op.needed = True
            n, ep = 0, -1
            for op in self.ops[e]:
                if op.epoch != ep:
                    n, ep = 0, op.epoch
                if op.needed and not op.isdma:
                    n += 1
                    op.signum = n
                    self.final[ep][e] = n

    def emit(self, eng, e, sems, dsems):
        known = {}
        last_on_slot = {}
        cur_ep = 0

        def barrier(b):
            for f, n in self.final[b].items():
                if known.get(f, 0) < n:
                    e.wait_ge(sems[f][b % 4], n)
            if b >= 2 and eng in sems:
                e.sem_clear(sems[eng][(b - 2) % 4])

        for op in self.ops[eng]:
            while cur_ep < op.epoch:
                barrier(cur_ep)
                cur_ep += 1
                known = {k: v for k, v in known.items() if isinstance(k, tuple)}
            waits = []
            for k, d in op.deps.items():
                if d.isdma:
                    waits.append((k, dsems[d.eng][d.slot], d.dval))
                else:
                    waits.append((k, sems[d.eng][op.epoch % 4], d.signum))
            if op.isdma:
                prev = last_on_slot.get(op.slot)
                if prev is not None:
                    waits.append((("dma", eng, op.slot), dsems[eng][op.slot], prev.dval))
                last_on_slot[op.slot] = op
            for k, sem, val in waits:
                if known.get(k, 0) >= val:
                    continue
                e.wait_ge(sem, val)
                known[k] = val
            ins = op.fn(e)
            if op.isdma:
                ins.then_inc(dsems[eng][op.slot], 16)
            elif op.needed:
                ins.then_inc(sems[eng][op.epoch % 4], 1)
        while cur_ep <= self.epoch:
            barrier(cur_ep)
            cur_ep += 1
        if eng in dsems:
            for slot, op in last_on_slot.items():
                e.wait_ge(dsems[eng][slot], op.dval)


D = 2048
KT = 16
T = 512
NCH = 4
D_IN = 6096
Z0, XS0, B0, C0, DT0 = 0, 1024, 2048, 2304, 2560
R0, K0, V0, PW0, PA0, PG0 = 2576, 3600, 4624, 5648, 5744, 5840
EPS = 1e-6
LN_EPS = 64e-5
ESQ = float(np.exp(-0.5))

C_ID, C_UT, C_SL, C_ONE, C_BLK, C_M1, C_SL64, C_ID64, C_RST = range(9)
NCST = 13


def make_consts():
    p = np.arange(128)[:, None]
    f = np.arange(128)[None, :]
    c = np.zeros((128, NCST * 128), np.float32)
    c[:, C_ID * 128:(C_ID + 1) * 128] = (p == f)
    c[:, C_UT * 128:(C_UT + 1) * 128] = (p <= f)
    c[:, C_SL * 128:(C_SL + 1) * 128] = (p > f)
    c[:, C_ONE * 128:(C_ONE + 1) * 128] = 1.0
    c[:, C_BLK * 128:(C_BLK + 1) * 128] = ((p // 64) == (f // 64))
    s = p % 64
    t = np.arange(64)[None, :]
    c[:, C_M1 * 128:C_M1 * 128 + 64] = (s < t)
    c[:, C_M1 * 128 + 64:(C_M1 + 1) * 128] = (s <= t)
    c[:, C_SL64 * 128:C_SL64 * 128 + 64] = (s > t)
    c[:, C_ID64 * 128:C_ID64 * 128 + 64] = (s == t)
    rst = np.ones((128, 512), np.float32)
    rst[:, ::64] = 0.0
    c[:, C_RST * 128:C_RST * 128 + 512] = rst
    return c


class Builder:
    def __init__(self, n_pre, n_own, dbg=None):
        self.n_pre, self.n_own = n_pre, n_own
        self.dbg = dbg or []
        self.S = Sched()
        self.nc = bass.Bass("TRN2", target_bir_lowering=False)
        self.rot = {}
        self.dram = {}
        self.sb = {}

    def din(self, name, shape, dtype=F32):
        t = self.nc.dram_tensor(name, list(shape), dtype, kind="ExternalInput").ap()
        self.dram[name] = t
        return t

    def dout(self, name, shape, dtype=F32):
        t = self.nc.dram_tensor(name, list(shape), dtype, kind="ExternalOutput").ap()
        self.dram[name] = t
        return t

    def alloc(self, stack, name, shape, dtype):
        t = stack.enter_context(self.nc.sbuf_tensor(name, list(shape), dtype))
        self.sb[name] = t
        return t

    def bank(self, grp):
        lo, n = {"m": (0, 4), "x": (4, 4)}[grp]
        i = self.rot.get(grp, 0)
        self.rot[grp] = i + 1
        return lo + i % n

    def psf(self, b, n=512, p0=0, p1=128, c0=0):
        return V(self.ps[p0:p1, b, c0:c0 + n], ("ps", b))

    def psb(self, b, n=1024, p0=0, p1=128, c0=0):
        return V(self.ps[p0:p1, b, :].bitcast(BF16)[:, c0:c0 + n], ("ps", b))

    def mm(self, out, lhsT, rhs, start=True, stop=True):
        o, l, r = out.ap, lhsT.ap, rhs.ap
        self.S.add("pe", lambda e: e.matmul(o, l, r, start=start, stop=stop), r=[lhsT, rhs], w=[out])

    def tr(self, out, in_, ident):
        o, i, d = out.ap, in_.ap, ident.ap
        self.S.add("pe", lambda e: e.transpose(o, i, d), r=[in_, ident], w=[out])

    def act(self, out, in_, func, bias=None, scale=None, accum=None):
        o, i = out.ap, in_.ap
        rs = [in_]
        ws = [out]
        kw = {}
        if bias is not None:
            if isinstance(bias, V):
                rs.append(bias)
                kw["bias"] = bias.ap
            else:
                kw["bias"] = float(bias)
        if scale is not None:
            if isinstance(scale, V):
                rs.append(scale)
                kw["scale"] = scale.ap
            else:
                kw["scale"] = float(scale)
        if accum is not None:
            ws.append(accum)
            kw["accum_out"] = accum.ap
        self.S.add("act", lambda e: e.activation(o, i, func, **kw), r=rs, w=ws)

    def tt(self, out, in0, in1, op, eng="dve"):
        o, a, b = out.ap, in0.ap, in1.ap
        self.S.add(eng, lambda e: e.tensor_tensor(o, a, b, op), r=[in0, in1], w=[out])

    def ts(self, out, in0, s1, op0, s2=None, op1=None, eng="dve"):
        o, a = out.ap, in0.ap
        rs = [in0]
        a1 = s1
        if isinstance(s1, V):
            rs.append(s1)
            a1 = s1.ap
        a2 = s2
        if isinstance(s2, V):
            rs.append(s2)
            a2 = s2.ap
        if op1 is None:
            self.S.add(eng, lambda e: e.tensor_scalar(o, a, a1, None, op0), r=rs, w=[out])
        else:
            self.S.add(eng, lambda e: e.tensor_scalar(o, a, a1, a2, op0, op1), r=rs, w=[out])

    def stt(self, out, in0, sc, in1, op0, op1):
        o, a, b = out.ap, in0.ap, in1.ap
        rs = [in0, in1]
        s = sc
        if isinstance(sc, V):
            rs.append(sc)
            s = sc.ap
        self.S.add("dve", lambda e: e.scalar_tensor_tensor(o, a, s, b, op0, op1), r=rs, w=[out])

    def cp(self, out, in_, eng="dve"):
        o, i = out.ap, in_.ap
        if eng == "act":
            self.S.add("act", lambda e: e.copy(o, i), r=[in_], w=[out])
        else:
            self.S.add(eng, lambda e: e.tensor_copy(o, i), r=[in_], w=[out])

    def recip(self, out, in_):
        o, i = out.ap, in_.ap
        self.S.add("dve", lambda e: e.reciprocal(o, i), r=[in_], w=[out])

    def memset(self, out, val, eng="dve"):
        o = out.ap
        self.S.add(eng, lambda e: e.memset(o, val), w=[out])

    def dma(self, out, in_, q="sp", **kw):
        o = out.ap if isinstance(out, V) else out
        i = in_.ap if isinstance(in_, V) else in_
        rs = [in_] if isinstance(in_, V) else []
        ws = [out] if isinstance(out, V) else []
        return self.S.add(q, lambda e: e.dma_start(out=o, in_=i, **kw), r=rs, w=ws, dma=True)

    def load_w(self, w2d, k0, c0, ncols):
        i = self.rot.get("w", 0)
        self.rot["w"] = i + 1
        slot = i % 2
        dst = V(self.wbuf[:, slot, :, 0:ncols], ("wbuf", slot))
        key = (id(w2d), k0, c0, ncols)
        use = self.use_scratch and ncols >= 128
        if use and key in self.wscr:
            off = self.wscr[key]
            src = self.scr[:, off:off + KT * ncols].rearrange("p (kt c) -> p kt c", c=ncols)
            self.dma(dst, V(src, ("scr", key)), q="sp")
            return slot
        src = w2d[k0:k0 + 2048, c0:c0 + ncols].rearrange("(kt p) c -> p kt c", p=128)
        self.dma(dst, src, q="pool")
        if use:
            off = self.scr_off
            self.scr_off += KT * ncols
            assert self.scr_off <= self.SCR_N
            self.wscr[key] = off
            self.dma(V(self.scr[:, off:off + KT * ncols].rearrange("p (kt c) -> p kt c", c=ncols), ("scr", key)), dst, q="sp")
        return slot

    def wv(self, slot, kt, c0, n, rows=128):
        return V(self.wbuf[0:rows, slot, kt, c0:c0 + n], ("wbuf", slot))

    def hTv(self, kt, c0=0, n=T):
        return V(self.hT[:, kt, c0:c0 + n], ("hT", kt))

    def lin_fm(self, w2d, c0, tiles, evac, k0=0, rhs=None):
        ncols = max(o + w for o, w in tiles)
        slot = self.load_w(w2d, k0, c0, ncols)
        rhs = rhs or self.hTv
        for (off, width) in tiles:
            b = self.bank("m")
            out = self.psf(b, T, 0, width)
            for kt in range(KT):
                self.mm(out, self.wv(slot, kt, off, width), rhs(kt), start=(kt == 0), stop=(kt == KT - 1))
            evac(off, width, out)

    def lin_tm(self, w2d, c0, ncols, lhs, evac, k0=0):
        slot = self.load_w(w2d, k0, c0, ncols)
        for c in range(NCH):
            b = self.bank("m")
            out = self.psf(b, ncols)
            for kt in range(KT):
                self.mm(out, lhs(kt, c), self.wv(slot, kt, 0, ncols), start=(kt == 0), stop=(kt == KT - 1))
            evac(c, out)

    def rmsnorm_hT(self, src_c, gpp, nch=NCH, dst=None, width=D):
        nkt = width // 128
        for c in range(nch):
            x = src_c(c)
            junk = V(self.junk[:, 0:width], ("hn", None))
            ss = V(self.sm[:, 0:1], ("sm", 0))
            self.act(junk, x, AF.Square, accum=ss)
            sq = V(self.sm[:, 1:2], ("sm", 1))
            self.act(sq, ss, AF.Sqrt, bias=self.eps_v, scale=1.0 / width)
            rstd = V(self.sm[:, 2:3], ("sm", 2))
            self.recip(rstd, sq)
            hn = V(self.hn[:, 0:width], ("hn", None))
            self.act(hn, x, AF.Copy, scale=rstd)
            for half in range(nkt // 8):
                b = self.bank("x")
                for j in range(8):
                    kt = half * 8 + j
                    self.tr(self.psb(b, 128, c0=j * 128), V(self.hn[:, kt * 128:(kt + 1) * 128], ("hn", None)), self.identb)
                g = V(gpp.ap[:, half * 8:half * 8 + 8].unsqueeze(2).to_broadcast([128, 8, 128]), gpp.keys)
                if dst is None:
                    o = V(self.hT[:, half * 8:half * 8 + 8, c * 128:(c + 1) * 128], [("hT", half * 8 + j) for j in range(8)])
                else:
                    o = dst(half, c)
                pin = V(self.psb(b).ap.rearrange("p (a b) -> p a b", b=128), ("ps", b))
                self.tt(o, pin, g, ALU.mult)

    def build(self):
        import contextlib
        nc = self.nc
        n_pre, n_own = self.n_pre, self.n_own
        xo = self.din("xo", [n_own * T, D])
        xp = self.din("xp", [max(n_pre, 1) * T, D])
        flag = self.din("flag", [128, 1])
        mem = self.din("mem", [256, D])
        cst = self.din("cst", [128, NCST * 128])
        w_in = self.din("w_in", [D, D_IN])
        w_out = self.din("w_out", [D, D])
        wq = self.din("wq", [D, D])
        wk = self.din("wk", [D, D])
        wvv = self.din("wv", [D, D])
        wo = self.din("wo", [D, D])
        w1 = self.din("w1", [D, 4 * D])
        w2 = self.din("w2", [4 * D, D])
        pp = self.din("pp", [128, 256])
        bc = self.din("bc", [1, 16 * 3 + 1024 + 2048])
        lora = self.din("lora", [96 * 2 + 256, 1024])
        yout = self.dout("y", [n_own * T, D])
        self.SCR_N = (D * D_IN + 3 * D * D + 8 * D * D) // 128 + 16 * 4096
        self.scr = nc.dram_tensor("wscr", [128, self.SCR_N], BF16).ap()
        self.scr_off = 0
        self.wscr = {}
        self.use_scratch = False
        dbg_out = {}
        for name, shape, dt_ in self.dbg:
            dbg_out[name] = self.dout("dbg_" + name, shape, dt_)

        with contextlib.ExitStack() as st:
            A = lambda name, shape, dt: self.alloc(st, name, shape, dt)
            self.ps = st.enter_context(nc.psum_tensor("ps", [128, 8, 512], F32))
            self.xres = A("xres", [128, NCH, D], F32)
            self.hT = A("hT", [128, KT, T], BF16)
            self.wbuf = A("wbuf", [128, 2, KT, 512], BF16)
            self.cstb = A("cstb", [128, NCST * 128], BF16)
            self.ppt = A("ppt", [128, 256], F32)
            self.bct = A("bct", [128, 48 + 1024], F32)
            self.lorab = A("lorab", [128, 4, 1024], BF16)
            self.kT = A("kT", [128, KT, 256], BF16)
            self.Vm = A("Vm", [128, 2, D], BF16)
            self.Sst = A("Sst", [128, 1024], F32)
            self.Sbf = A("Sbf", [128, 1024], BF16)
            self.Tst = A("Tst", [128, 8, 64], F32)
            self.Tbf = A("Tbf", [128, 8, 64], BF16)
            self.convh = A("convh", [128, 12, 3], F32)
            self.shifth = A("shifth", [128, 28], F32)
            self.sm = A("sm", [128, 64], F32)
            self.flagt = A("flagt", [128, 1], F32)
            self.hn = A("hn", [128, D], BF16)
            self.junk = self.hn
            self.ymixT = A("ymixT", [128, KT, T], BF16)
            self.ARENA = 16832
            self.arena = A("arena", [128, self.ARENA], F32)

            def cstv(blk, n=128, rows=128, off=0):
                return V(self.cstb[0:rows, blk * 128 + off:blk * 128 + off + n], ("cstb", None))
            self.identb = cstv(C_ID)
            self.eps_v = V(self.ppt[:, 255:256], ("ppt", None))

            sems = {}
            dsems = {}
            for e in ["pe", "act", "dve", "pool"]:
                sems[e] = [st.enter_context(nc.semaphore("s_%s%d" % (e, i))) for i in range(4)]
            for q, n in self.S.ndma.items():
                dsems[q] = [st.enter_context(nc.semaphore("d_%s%d" % (q, i))) for i in range(n)]

            self.cstv = cstv
            self.emit_program(xo, xp, flag, mem, cst, w_in, w_out, wq, wk, wvv, wo, w1, w2, pp, bc, lora, yout, dbg_out)
            self.S.finalize()

            with nc.Block() as block:
                @block.tensor
                def _(e):
                    self.S.emit("pe", e, sems, dsems)

                @block.scalar
                def _(e):
                    self.S.emit("act", e, sems, dsems)

                @block.vector
                def _(e):
                    self.S.emit("dve", e, sems, dsems)

                @block.gpsimd
                def _(e):
                    self.S.emit("pool", e, sems, dsems)

                @block.sync
                def _(e):
                    self.S.emit("sp", e, sems, dsems)
        return nc

    def carve4(self, off, name, shape, dtype):
        a, o = self.carve(off, name, [shape[0] * shape[1], shape[2], shape[3]], dtype)
        return a.rearrange("p (u q) a b -> p u q a b", u=shape[0]), o

    def begin_phase(self):
        pend = {}
        for name in getattr(self, "arena_names", set()):
            ent = self.S.state.pop(name, {})
            for st_ in ent.values():
                for d in ([st_.writer] if st_.writer is not None else []) + st_.readers:
                    k = ("dma", d.eng, d.slot) if d.isdma else d.eng
                    cur = pend.get(k)
                    if cur is None or (d.dval if d.isdma else d.idx) > (cur.dval if cur.isdma else cur.idx):
                        pend[k] = d
        for d in getattr(self, "pending", []):
            k = ("dma", d.eng, d.slot) if d.isdma else d.eng
            cur = pend.get(k)
            if cur is None or (d.dval if d.isdma else d.idx) > (cur.dval if cur.isdma else cur.idx):
                pend[k] = d
        self.pending = list(pend.values())
        self.arena_names = set()

    def carve(self, off, name, shape, dtype, at=None):
        if name not in self.arena_names:
            self.arena_names.add(name)
            st_ = St()
            st_.readers = list(self.pending)
            self.S.state[name] = {None: st_}
        n = int(np.prod(shape))
        if at is not None:
            words = n if dtype == F32 else (n + 1) // 2
            a = self.arena[:, at:at + words]
            if dtype != F32:
                a = a.bitcast(BF16)[:, 0:n]
            if len(shape) == 2:
                a = a.rearrange("p (a b) -> p a b", b=shape[1])
            elif len(shape) == 3:
                a = a.rearrange("p (a b c) -> p a b c", b=shape[1], c=shape[2])
            return a, off
        words = n if dtype == F32 else (n + 1) // 2
        a = self.arena[:, off:off + words]
        if dtype != F32:
            a = a.bitcast(BF16)[:, 0:n]
        if len(shape) == 2:
            a = a.rearrange("p (a b) -> p a b", b=shape[1])
        elif len(shape) == 3:
            a = a.rearrange("p (a b c) -> p a b c", b=shape[1], c=shape[2])
        assert off + words <= self.ARENA, (name, off + words)
        return a, off + words

    def emit_program(self, xo, xp, flag, mem, cst, w_in, w_out, wq, wk, wvv, wo, w1, w2, pp, bc, lora, yout, dbg_out):
        S = self.S
        self.dbg_out = dbg_out
        self.dma(V(self.cstb[:, :], ("cstb", None)), cst, q="pool")
        self.dma(V(self.ppt[:, :], ("ppt", None)), pp)
        self.dma(V(self.bct[:, :], ("bct", None)), bc.partition_broadcast(128)[:, 0, 0:48 + 1024])
        self.bc_dram = bc
        self.dma(V(self.flagt[:, :], ("flagt", None)), flag)
        self.dma(V(self.lorab[0:96, 0, :], ("lorab", 0)), lora[0:96, :], q="pool")
        self.dma(V(self.lorab[0:96, 1, :], ("lorab", 1)), lora[96:192, :], q="pool")
        self.dma(V(self.lorab[:, 2:4, :], ("lorab", 2)), lora[192:448, :].rearrange("(k p) c -> p k c", p=128), q="pool")
        alog = V(self.bct[:, 0:16], ("bct", "a"))
        self.act(alog, V(self.bct[:, 0:16], ("bct", None)), AF.Exp)
        self.ts(alog, alog, -1.0, ALU.mult)
        self.a_bc = alog
        self.dtb_bc = V(self.bct[:, 16:32], ("bct", None))
        self.D_bc = V(self.bct[:, 32:48], ("bct", None))
        self.ng_bc = V(self.bct[:, 48:48 + 1024], ("bct", None))
        omka = V(self.ppt[:, self.PP_OMKA:self.PP_OMKA + 8], ("ppt", "omka"))
        self.ts(omka, V(self.ppt[:, self.PP_KA:self.PP_KA + 8], ("ppt", None)), -1.0, ALU.mult, 1.0, ALU.add)
        for t, k in [(self.Sst, "Sst"), (self.Sbf, "Sbf"), (self.Tst, "Tst"), (self.Tbf, "Tbf"), (self.convh, "convh"), (self.shifth, "shifth")]:
            self.memset(V(t[:], (k, None)), 0.0)

        self.w_in, self.w_out, self.wq, self.wo, self.w1, self.w2 = w_in, w_out, wq, wo, w1, w2
        self.mem_kv(mem, wk, wvv)
        self.use_scratch = True
        tiles = [(xp, i, True) for i in range(self.n_pre)] + [(xo, i, False) for i in range(self.n_own)]
        self.conv_list = []
        if self.n_pre > 0:
            for wmat in (self.w_out, self.wq, self.wo):
                self.conv_list += [(wmat, 0, blk * 512, 512) for blk in range(4)]
            for qt in range(4):
                self.conv_list += [(self.w1, 0, qt * 2048 + blk * 512, 512) for blk in range(4)]
                self.conv_list += [(self.w2, qt * 2048, blk * 512, 512) for blk in range(4)]
        ncall = 2 * max(self.n_pre - 1, 1)
        self.conv_per_call = (len(self.conv_list) + ncall - 1) // ncall
        self.conv_skip = 2 if self.n_pre > 1 else 0
        self.load_x(*tiles[0][:2])
        for ti, (xsrc, i, so_) in enumerate(tiles):
            self.S.new_epoch()
            if so_:
                self.mixer(state_only=True, last=(i == self.n_pre - 1))
                if i == self.n_pre - 1:
                    self.apply_flag()
                if ti + 1 < len(tiles):
                    self.load_x(*tiles[ti + 1][:2])
            else:
                self.mixer(state_only=False)
                self.out_proj()
                self.xattn()
                self.ffn()
                self.final(yout, i, tiles[ti + 1][:2] if ti + 1 < len(tiles) else None)

    def ppv(self, c0, n=1):
        return V(self.ppt[:, c0:c0 + n], ("ppt", None))

    PP_G1, PP_G2, PP_G3, PP_GM = 0, 16, 32, 48
    PP_CW, PP_CB, PP_MU = 64, 112, 124
    PP_W0, PP_A0, PP_KK, PP_KA, PP_OMKA, PP_RK, PP_LNW, PP_LNB = 152, 160, 168, 176, 184, 192, 200, 208

    def load_x(self, xsrc, i):
        for c in range(NCH):
            self.dma(V(self.xres[:, c, :], ("xres", c)), xsrc[i * T + c * 128:i * T + (c + 1) * 128, :])

    def xc(self, c):
        return V(self.xres[:, c, :], ("xres", c))

    def apply_flag(self):
        fl = V(self.flagt[:, 0:1], ("flagt", None))
        for t, k in [(self.Sst, "Sst"), (self.Tst, "Tst"), (self.convh, "convh"), (self.shifth, "shifth")]:
            v = V(t[:], (k, None))
            self.ts(v, v, fl, ALU.mult)
        self.cp(V(self.Sbf[:], ("Sbf", None)), V(self.Sst[:], ("Sst", None)), eng="act")
        self.cp(V(self.Tbf[:], ("Tbf", None)), V(self.Tst[:], ("Tst", None)), eng="act")

    def mem_kv(self, mem, wk, wvv):
        self.begin_phase()
        mt, o = self.carve(0, "memt", [2, D], F32)
        for c in range(2):
            self.dma(V(mt[:, c, :], ("memt", c)), mem[c * 128:(c + 1) * 128, :])
        mT, o = self.carve(o, "mT", [KT, 256], BF16)

        def dst(half, c):
            return V(mT[:, half * 8:half * 8 + 8, c * 128:(c + 1) * 128], ("mT", None))
        self.rmsnorm_hT(lambda c: V(mt[:, c, :], ("memt", c)), self.ppv(self.PP_GM, 16), nch=2, dst=dst)
        rhs = lambda kt: V(mT[:, kt, :], ("mT", None))
        for blk in range(4):
            slot = self.load_w(wk, 0, blk * 512, 512)
            for ct in range(4):
                b = self.bank("m")
                out = self.psf(b, 256)
                for kt in range(KT):
                    self.mm(out, self.wv(slot, kt, ct * 128, 128), rhs(kt), start=(kt == 0), stop=(kt == KT - 1))
                self.cp(V(self.kT[:, blk * 4 + ct, :], ("kT", None)), out, eng="act")
        for blk in range(4):
            slot = self.load_w(wvv, 0, blk * 512, 512)
            for c in range(2):
                b = self.bank("m")
                out = self.psf(b, 512)
                for kt in range(KT):
                    self.mm(out, V(mT[:, kt, c * 128:(c + 1) * 128], ("mT", None)), self.wv(slot, kt, 0, 512),
                            start=(kt == 0), stop=(kt == KT - 1))
                self.cp(V(self.Vm[:, c, blk * 512:(blk + 1) * 512], ("Vm", None)), out, eng="act")

    def mixer(self, state_only, last=False):
        self.rmsnorm_hT(self.xc, self.ppv(self.PP_G1, 16))
        self.halo_all = (not state_only) or last
        self.ssd(state_only)
        self.rwkv(state_only)

    def debug_dump(self, name, v):
        if name in self.dbg_out:
            self.dma(self.dbg_out[name], v)

    def ssd(self, so):
        self.begin_phase()
        o = 0
        xbc, o = self.carve(o, "xbc", [12, T], BF16)
        sz, o = self.carve(o, "sz", [NCH, 1024], BF16)
        o_pre = o
        pre, o = self.carve(o, "pre", [2, 3 + T], F32)
        o_acc = o
        acc, o = self.carve(o, "acc", [2, T], F32)
        dtr, o = self.carve(o, "dtr", [NCH, 16], F32)
        dah, o = self.carve(o, "dah", [NCH, 16], BF16)
        dal, o = self.carve(o, "dal", [NCH, 16], BF16)
        xs_tok, o = self.carve(o, "xs_tok", [2, 1024], BF16)
        B_tok, o = self.carve(o, "B_tok", [2, 256], BF16)
        o_rch = o
        rch, o = self.carve(o, "rch", [8, 128], BF16)
        rcl, o = self.carve(o, "rcl", [8, 128], BF16)
        o_LT = o
        LT, o = self.carve(o, "LT", [8, 128], F32)
        MT, o = self.carve(o, "MT", [32, 128], BF16)
        cbm, o = self.carve(o, "cbm", [2, 128], F32)
        xdt0, o = self.carve(o, "xd", [16, 64], BF16, at=o_pre)
        xdd0, o = self.carve(o, "xd", [16, 64], BF16, at=o_pre + 512)
        xdt1, o = self.carve(o, "xdt1", [16, 64], BF16)
        xdd1, o = self.carve(o, "xdd1", [16, 64], BF16)
        xdt, xdd = [xdt0, xdt1], [xdd0, xdd1]
        yy, o = self.carve(o, "yy", [16, 64], F32, at=o_acc)
        tmp, o = self.carve(o, "tmp", [16, 64], F32)
        ysn, o = self.carve(o, "ysn", [1, 1024], BF16)
        sm2, o = self.carve(o, "sm2", [2, 128], F32)
        w_in = self.w_in
        UT, SL, ONE = self.cstv(C_UT), self.cstv(C_SL), self.cstv(C_ONE)

        slot = self.load_w(w_in, 0, DT0, 16)
        for c in range(NCH):
            b = self.bank("x")
            out = self.psf(b, 16)
            for kt in range(KT):
                self.mm(out, self.hTv(kt, c * 128, 128), self.wv(slot, kt, 0, 16), start=(kt == 0), stop=(kt == KT - 1))
            d = V(dtr[:, c, :], ("dtr", c))
            self.tt(d, out, self.dtb_bc, ALU.add)
            self.act(d, d, AF.Exp)
            self.act(d, d, AF.Ln, bias=1.0)
        dall = V(dtr[:, :, :], ("dtr", None))
        da, _ = self.carve(o, "da", [NCH, 16], F32)
        o2 = o + NCH * 16
        dav = V(da[:, :, :], ("da", None))
        self.tt(dav, dall, V(self.a_bc.ap.unsqueeze(1).to_broadcast([128, NCH, 16]), self.a_bc.keys), ALU.mult)
        dahv = V(dah[:, :, :], ("dah", None))
        dalv = V(dal[:, :, :], ("dal", None))
        self.cp(dahv, dav)
        self.tt(dalv, dav, dahv, ALU.subtract)

        def evac_xbc(base_ft):
            def f(off, width, ps):
                ft = base_ft + off // 128
                i = self.rot.get("pre", 0)
                self.rot["pre"] = i + 1
                pb = i % 2
                pv = lambda a, n: V(pre[:, pb, a:a + n], ("pre", pb))
                hal = V(self.convh[:, ft, :], ("convh", ft))
                self.cp(pv(0, 3), hal)
                self.cp(pv(3, T), ps, eng="act")
                self.cp(hal, pv(T, 3))
                av = V(acc[:, pb, :], ("acc", pb))
                cw = lambda k: self.ppv(self.PP_CW + ft * 4 + k)
                self.ts(av, pv(0, T), cw(0), ALU.mult)
                for k in range(1, 4):
                    self.stt(av, pv(k, T), cw(k), av, ALU.mult, ALU.add)
                self.act(V(xbc[:, ft, :], ("xbc", ft)), av, AF.Silu, bias=self.ppv(self.PP_CB + ft))
            return f
        self.lin_fm(w_in, XS0, [(i * 128, 128) for i in range(4)], evac_xbc(0))
        self.lin_fm(w_in, XS0 + 512, [(i * 128, 128) for i in range(4)], evac_xbc(4))
        self.lin_fm(w_in, B0, [(i * 128, 128) for i in range(4 if self.halo_all else 2)], evac_xbc(8))

        if not so:
            for blk in range(2):
                def evz(c, ps, blk=blk):
                    self.act(V(sz[:, c, blk * 512:(blk + 1) * 512], ("sz", c)), ps, AF.Silu)
                self.lin_tm(w_in, Z0 + blk * 512, 512, lambda kt, c: self.hTv(kt, c * 128, 128), evz)

        bc16 = lambda v: V(v.ap.unsqueeze(2).to_broadcast([128, 16, 64]), v.keys)

        def prep_stages(c):
            par = c % 2
            cs = slice(c * 128, (c + 1) * 128)
            XK, BK, SK, MK = ("xs_tok", par), ("B_tok", par), ("sm2", par), ("MT", par)
            DK = ("pre", None) if par == 0 else ("xdt1", None)
            dtc = V(dtr[:, c, :], ("dtr", c))
            xs3 = V(xs_tok[:, par, :].rearrange("p (h q) -> p h q", q=64), XK)

            def p_tr():
                b = self.bank("x")
                for j in range(8):
                    self.tr(self.psb(b, 128, c0=j * 128), V(xbc[:, j, cs], ("xbc", j)), self.identb)
                self.cp(V(xs_tok[:, par, :], XK), self.psb(b, 1024), eng="act")
                b = self.bank("x")
                for j in range(2):
                    self.tr(self.psb(b, 128, c0=j * 128), V(xbc[:, 8 + j, cs], ("xbc", 8 + j)), self.identb)
                self.cp(V(B_tok[:, par, :], BK), self.psb(b, 256), eng="act")

            def p_cs():
                dh = V(dah[:, c, :], ("dah", None))
                dl = V(dal[:, c, :], ("dal", None))
                b = self.bank("x")
                cs_ps = self.psf(b, 16)
                self.mm(cs_ps, UT, dh, start=True, stop=False)
                self.mm(cs_ps, UT, dl, start=False, stop=True)
                tot_ps = self.psf(b, 16, c0=16)
                self.mm(tot_ps, ONE, dh, start=True, stop=False)
                self.mm(tot_ps, ONE, dl, start=False, stop=True)
                cst_ = V(sm2[:, par, 0:16], SK)
                ecs = V(sm2[:, par, 16:32], SK)
                dte = V(sm2[:, par, 32:48], SK)
                cd = V(sm2[:, par, 48:64], SK)
                self.cp(cst_, cs_ps, eng="act")
                self.act(ecs, cs_ps, AF.Exp)
                self.act(cd, tot_ps, AF.Exp)
                self.tt(dte, tot_ps, cst_, ALU.subtract)
                self.act(dte, dte, AF.Exp)

            def p_x():
                dte = V(sm2[:, par, 32:48], SK)
                self.tt(V(xdt[par][:, :, :], DK), xs3, bc16(dtc), ALU.mult, eng="pool")
                self.tt(V(xdd[par][:, :, :], DK), V(xdt[par][:, :, :], DK), bc16(dte), ALU.mult, eng="pool")

            def p_g(g):
                rchv = V(rch[:, :, :], ("rch", None))
                rclv = V(rcl[:, :, :], ("rch", None))
                utb = V(UT.ap.unsqueeze(1).to_broadcast([128, 8, 128]), UT.keys)
                dhb = V(dah[:, c, g * 8:(g + 1) * 8].unsqueeze(2).to_broadcast([128, 8, 128]), ("dah", None))
                dlb = V(dal[:, c, g * 8:(g + 1) * 8].unsqueeze(2).to_broadcast([128, 8, 128]), ("dal", None))
                self.tt(rchv, utb, dhb, ALU.mult, eng="pool")
                self.tt(rclv, utb, dlb, ALU.mult, eng="pool")
                for hf in range(2):
                    b = self.bank("x")
                    sp = self.psf(b, 512)
                    self.mm(sp, SL, V(rch[:, hf * 4:(hf + 1) * 4, :], ("rch", None)), start=True, stop=False)
                    self.mm(sp, SL, V(rcl[:, hf * 4:(hf + 1) * 4, :], ("rch", None)), start=False, stop=True)
                    self.act(V(LT[:, hf * 4:(hf + 1) * 4, :], ("LT", None)), sp, AF.Exp)
                b = self.bank("x")
                cbp = self.psf(b, 128)
                self.mm(cbp, V(xbc[:, 8 + g, cs], ("xbc", 8 + g)), V(xbc[:, 10 + g, cs], ("xbc", 10 + g)))
                cbv = V(cbm[:, g, :], ("cbm", g))
                self.tt(cbv, cbp, UT, ALU.mult)
                self.tt(V(MT[:, par * 16 + g * 8:par * 16 + (g + 1) * 8, :], MK), V(LT[:, :, :], ("LT", None)),
                        V(cbm[:, g, :].unsqueeze(1).to_broadcast([128, 8, 128]), ("cbm", g)), ALU.mult)
            st = [p_tr, p_cs, p_x]
            if not so:
                st += [lambda: p_g(0), lambda: p_g(1)]
            return st

        def fin_stages(c):
            par = c % 2
            cs = slice(c * 128, (c + 1) * 128)
            XK, BK, SK, MK = ("xs_tok", par), ("B_tok", par), ("sm2", par), ("MT", par)
            DK = ("pre", None) if par == 0 else ("xdt1", None)
            xs3 = V(xs_tok[:, par, :].rearrange("p (h q) -> p h q", q=64), XK)
            yv = V(yy[:, :, :], ("acc", None))
            tv = V(tmp[:, :, :], ("tmp", None))
            stt_ = {}

            def f_y():
                by = [self.bank("m"), self.bank("m")]
                for h in range(16):
                    self.mm(V(self.ps[:, by[h // 8], (h % 8) * 64:(h % 8 + 1) * 64], ("ps", by[h // 8])),
                            V(MT[:, par * 16 + h, :], MK), V(xdt[par][:, h, :], DK))
                bo = [self.bank("m"), self.bank("m")]
                for g in range(2):
                    self.mm(self.psf(bo[g], 512), V(xbc[:, 10 + g, cs], ("xbc", 10 + g)),
                            V(self.Sbf[:, g * 512:(g + 1) * 512], ("Sbf", None)))
                for g in range(2):
                    tg = V(tmp[:, g * 8:(g + 1) * 8, :], ("tmp", None))
                    yg = V(yy[:, g * 8:(g + 1) * 8, :], ("acc", None))
                    eg = V(sm2[:, par, 16 + g * 8:16 + (g + 1) * 8].unsqueeze(2).to_broadcast([128, 8, 64]), SK)
                    self.tt(tg, V(self.ps[:, bo[g], :].rearrange("p (h q) -> p h q", q=64), ("ps", bo[g])), eg, ALU.mult)
                    self.tt(yg, V(self.ps[:, by[g], :].rearrange("p (h q) -> p h q", q=64), ("ps", by[g])), tg, ALU.add)

            def f_comb():
                self.tt(tv, xs3, bc16(self.D_bc), ALU.mult, eng="pool")
                self.tt(yv, yv, tv, ALU.add)
                y2 = V(yy[:, :, :].rearrange("p h q -> p (h q)"), ("acc", None))
                self.tt(y2, y2, V(sz[:, c, :], ("sz", c)), ALU.mult)

            def f_norm():
                for g in range(2):
                    self.act(V(self.junk[:, 0:512], ("hn", None)),
                             V(yy[:, g * 8:(g + 1) * 8, :].rearrange("p h q -> p (h q)"), ("acc", None)),
                             AF.Square, accum=V(sm2[:, par, 64 + g:65 + g], SK))
                rs2 = V(sm2[:, par, 64:66], SK)
                self.act(rs2, rs2, AF.Sqrt, bias=self.eps_v, scale=1.0 / 512)
                self.recip(rs2, rs2)
                for g in range(2):
                    self.stt(V(ysn[:, 0, g * 512:(g + 1) * 512], ("ysn", None)),
                             V(yy[:, g * 8:(g + 1) * 8, :].rearrange("p h q -> p (h q)"), ("acc", None)),
                             V(sm2[:, par, 64 + g:65 + g], SK),
                             V(self.ng_bc.ap[:, g * 512:(g + 1) * 512], self.ng_bc.keys), ALU.mult, ALU.mult)

            def f_out():
                b = self.bank("m")
                for j in range(8):
                    self.tr(self.psb(b, 128, c0=j * 128), V(ysn[:, 0, j * 128:(j + 1) * 128], ("ysn", None)), self.identb)
                self.cp(V(self.ymixT[:, 0:8, cs], [("ymixT", j) for j in range(8)]),
                        V(self.psb(b).ap.rearrange("p (a b) -> p a b", b=128), ("ps", b)), eng="act")

            def f_state():
                cd = V(sm2[:, par, 48:64], SK)
                bs = [self.bank("m"), self.bank("m")]
                for g in range(2):
                    self.mm(self.psf(bs[g], 512), V(B_tok[:, par, g * 128:(g + 1) * 128], BK),
                            V(xdd[par][:, g * 8:(g + 1) * 8, :], DK))
                S3 = V(self.Sst[:, :].rearrange("p (h q) -> p h q", q=64), ("Sst", None))
                self.tt(S3, S3, bc16(cd), ALU.mult)
                for g in range(2):
                    sg = V(self.Sst[:, g * 512:(g + 1) * 512], ("Sst", None))
                    self.tt(sg, sg, self.psf(bs[g], 512), ALU.add)
                self.cp(V(self.Sbf[:, :], ("Sbf", None)), V(self.Sst[:, :], ("Sst", None)), eng="act")
            if so:
                return [f_state]
            return [f_y, f_comb, f_norm, f_out, f_state]

        for c in range(NCH + 1):
            pr = prep_stages(c) if c < NCH else []
            fi = fin_stages(c - 1) if c > 0 else []
            for i in range(max(len(pr), len(fi))):
                if i < len(pr):
                    pr[i]()
                if i < len(fi):
                    fi[i]()
        if not so:
            self.debug_dump("ymix_ssd", V(self.ymixT[:, 0:8, :], ("ymixT", None)))
            self.debug_dump("Sst", V(self.Sst[:, :], ("Sst", None)))

    def rwkv(self, so):
        self.begin_phase()
        PG = 4 if so else 2
        o = 0
        tpw, o = self.carve(o, "tpw", [1, T], BF16)
        pab, o = self.carve(o, "pab", [1, T], BF16)
        spg, o = self.carve(o, "spg", [2, T], BF16)
        o_pre = o
        pre, o = self.carve(o, "pre", [2, 1 + T], F32)
        o_dd = o
        dd, o = self.carve(o, "dd", [2, T], F32)
        kraw, o = self.carve(o, "kraw", [4, T], F32)
        vb, o = self.carve(o, "vb", [4, T], BF16)
        rraw, o = self.carve(o, "rraw", [4, T], BF16)
        AR, o = self.carve(o, "AR", [1 if so else 2, PG, T], BF16)
        bt, o = self.carve(o, "bt", [PG, T], BF16)
        kt_, o = self.carve(o, "kt_", [PG, T], BF16)
        bE, o = self.carve(o, "bE", [PG, T], BF16)
        kE, o = self.carve(o, "kE", [PG, T], BF16)
        GC, o = self.carve(o, "GC", [PG, 8], F32)
        if not so:
            bon, o = self.carve(o, "bon", [PG, T], BF16)
            yr, o = self.carve(o, "yr", [PG, T], F32)
        f1, o = self.carve(o, "f1", [1, T], F32)
        f2, o = self.carve(o, "f2", [1, T], F32)
        f3, o = self.carve(o, "f3", [1, T], F32, at=o_pre)
        f4, o = self.carve(o, "f4", [1, T], F32, at=o_pre + 513)
        f5, o = self.carve(o, "f5", [1, T], F32, at=o_dd)
        f6, o = self.carve(o, "f6", [1, T], F32, at=o_dd + 512)
        h1, o = self.carve(o, "h1", [1, T], BF16)
        X1m, o = self.carve4(o, "X1m", [2, PG, 2, 64], BF16)
        X2m, o = self.carve4(o, "X2m", [2, PG, 2, 64], BF16)
        PTf, o = self.carve(o, "PTf", [2, PG, 64], BF16)
        ZP, o = self.carve(o, "ZP", [2, PG, 128], BF16)
        Lb, o = self.carve(o, "Lb", [2, PG, 64], BF16)
        TM, o = self.carve4(o, "TM", [2, PG, 3, 64], BF16)
        Xb, o = self.carve(o, "Xb", [PG, 64], BF16)
        Ub, o = self.carve(o, "Ub", [PG, 64], BF16)
        w_in = self.w_in
        BLK = self.cstv(C_BLK)

        def shift_evac(idx, rows, ps, dst, func=None):
            i = self.rot.get("pre", 0)
            self.rot["pre"] = i + 1
            pb = i % 2
            pv = lambda a, n: V(pre[0:rows, pb, a:a + n], ("pre", pb))
            hal = V(self.shifth[0:rows, idx:idx + 1], ("shifth", idx))
            self.cp(pv(0, 1), hal)
            self.cp(pv(1, T), ps, eng="act")
            self.cp(hal, pv(T, 1))
            dv = V(dd[0:rows, pb, :], ("dd", pb))
            self.tt(dv, pv(0, T), pv(1, T), ALU.subtract, eng="pool")
            mu = V(self.ppt[0:rows, self.PP_MU + idx:self.PP_MU + idx + 1], ("ppt", None))
            if func is None:
                self.stt(dst, dv, mu, pv(1, T), ALU.mult, ALU.add)
            else:
                self.stt(dv, dv, mu, pv(1, T), ALU.mult, ALU.add)
                self.act(dst, dv, func)

        def ev_pwpa(off, width, ps):
            if off == 0:
                shift_evac(24, 96, ps, V(tpw[0:96, 0, :], ("tpw", None)), func=AF.Tanh)
            else:
                shift_evac(25, 96, ps, V(pab[0:96, 0, :], ("pab", None)), func=AF.Copy)
        self.lin_fm(w_in, PW0, [(0, 96), (96, 96)], ev_pwpa)
        if self.halo_all:
            def ev_pg(off, width, ps):
                j = off // 128
                shift_evac(26 + j, 128, ps, V(spg[:, j, :], ("spg", j)), func=AF.Sigmoid)
            self.lin_fm(w_in, PG0, [(0, 128), (128, 128)], ev_pg)

        HH = [(0, 64), (64, 128)]
        bcq = lambda blk, n: V(self.cstb[:, blk * 128:blk * 128 + n].unsqueeze(1).to_broadcast([128, PG, n]), ("cstb", None))
        M1, M1s, SL64, ID64 = bcq(C_M1, 128), bcq(C_M1, 64), bcq(C_SL64, 64), bcq(C_ID64, 64)
        v3k = ("pre", None)
        v5k = ("dd", None)
        for gq in range(2):
            def ev_k(off, width, ps):
                fl = off // 128
                shift_evac(8 + gq * 4 + fl, 128, ps, V(kraw[:, fl, :], ("kraw", fl)))

            def ev_v(off, width, ps):
                fl = off // 128
                shift_evac(16 + gq * 4 + fl, 128, ps, V(vb[:, fl, :], ("vb", fl)))

            def ev_r(off, width, ps):
                fl = off // 128
                shift_evac(gq * 4 + fl, 128, ps, V(rraw[:, fl, :], ("rraw", fl)))
            t4 = [(i * 128, 128) for i in range(4)]
            self.lin_fm(w_in, K0 + gq * 512, t4, ev_k)
            self.lin_fm(w_in, V0 + gq * 512, t4, ev_v)
            if self.halo_all:
                self.lin_fm(w_in, R0 + gq * 512, t4, ev_r)
            if so:
                self.convert_some()
            for sg in range(4 // PG):
                for q in range(PG):
                    fl = sg * PG + q
                    f = gq * 4 + fl
                    v1, v2 = V(f1[:, 0, :], ("f1", None)), V(f2[:, 0, :], ("f2", None))
                    v3, v4 = V(f3[:, 0, :], v3k), V(f4[:, 0, :], v3k)
                    v5, v6 = V(f5[:, 0, :], v5k), V(f6[:, 0, :], v5k)
                    hv = V(h1[:, 0, :], ("h1", None))
                    kr = V(kraw[:, fl, :], ("kraw", fl))
                    b = self.bank("x")
                    pw_ps = self.psf(b, T)
                    self.mm(pw_ps, V(self.lorab[0:96, 0, f * 128:(f + 1) * 128], ("lorab", 0)), V(tpw[0:96, 0, :], ("tpw", None)))
                    self.act(v1, pw_ps, AF.Sigmoid, bias=self.ppv(self.PP_W0 + f))
                    self.ts(v1, v1, -ESQ, ALU.mult)
                    rstm = V(self.cstb[:, C_RST * 128:C_RST * 128 + T], ("cstb", None))
                    o1, a1, b1 = v2.ap, rstm.ap, v1.ap
                    self.S.add("dve", lambda e, o1=o1, a1=a1, b1=b1: e.tensor_tensor_scan(o1, a1, b1, 0.0, ALU.mult, ALU.add),
                               r=[rstm, v1], w=[v2])
                    b = self.bank("x")
                    pa_ps = self.psf(b, T)
                    self.mm(pa_ps, V(self.lorab[0:96, 1, f * 128:(f + 1) * 128], ("lorab", 1)), V(pab[0:96, 0, :], ("pab", None)))
                    self.act(v3, pa_ps, AF.Sigmoid, bias=self.ppv(self.PP_A0 + f))
                    self.ts(v4, kr, self.ppv(self.PP_KK + f), ALU.mult)
                    self.act(hv, v4, AF.Square)
                    b = self.bank("x")
                    ss_ps = self.psf(b, T)
                    self.mm(ss_ps, BLK, hv)
                    self.act(v5, ss_ps, AF.Sqrt)
                    self.ts(v5, v5, 1e-12, ALU.max)
                    self.recip(v5, v5)
                    self.tt(v4, v4, v5, ALU.mult)
                    self.ts(v5, v3, self.ppv(self.PP_KA + f), ALU.mult, self.ppv(self.PP_OMKA + f), ALU.add)
                    self.tt(v5, v5, kr, ALU.mult)
                    if not so:
                        rr = V(rraw[:, fl, :], ("rraw", fl))
                        self.tt(v6, rr, v5, ALU.mult)
                        self.ts(hv, v6, self.ppv(self.PP_RK + f), ALU.mult)
                        b = self.bank("x")
                        bo_ps = self.psf(b, T)
                        self.mm(bo_ps, BLK, hv)
                        self.tt(V(bon[:, q, :], ("bon", q)), bo_ps, V(vb[:, fl, :], ("vb", fl)), ALU.mult)
                    self.tt(v3, v3, v4, ALU.mult)
                    self.act(v6, v2, AF.Exp)
                    self.cp(V(GC[:, q, :], ("GC", q)), V(f6[:, 0, 63::64], v5k))
                    if not so:
                        self.tt(V(AR[:, 1, q, :], ("AR", q)), V(rraw[:, fl, :], ("rraw", fl)), v6, ALU.mult)
                    self.tt(v1, v2, v1, ALU.subtract)
                    self.act(v1, v1, AF.Exp)
                    self.stt(V(AR[:, 0, q, :], ("AR", q)), v4, -1.0, v1, ALU.mult, ALU.mult)
                    self.act(v1, v2, AF.Exp, scale=-1.0)
                    self.tt(V(bt[:, q, :], ("bt", q)), v3, v1, ALU.mult)
                    self.tt(V(kt_[:, q, :], ("kt_", q)), v5, v1, ALU.mult)
                    v1_3 = V(f1[:, 0, :].rearrange("p (j q) -> p j q", q=64), ("f1", None))
                    self.tt(v1_3, v1_3, V(GC[:, q, :].unsqueeze(2).to_broadcast([128, 8, 64]), ("GC", q)), ALU.mult)
                    self.tt(V(bE[:, q, :], ("bE", q)), v3, v1, ALU.mult)
                    self.tt(V(kE[:, q, :], ("kE", q)), v5, v1, ALU.mult)
                nar = 1 if so else 2
                ps3 = lambda bk, w: self.ps[:, bk, 0:PG * w].rearrange("p (q c) -> p q c", c=w)
                f0 = gq * 4 + sg * PG

                def inv_stages(j):
                    par = j % 2
                    c64 = slice(j * 64, (j + 1) * 64)
                    X1k, X2k, TMk, PTk = ("X1m", par), ("X2m", par), ("TM", par), ("PTf", par)

                    def s_tr():
                        bt_ = self.bank("x")
                        for q in range(PG):
                            fl = sg * PG + q
                            for (p0, p1) in HH:
                                idn = V(self.cstb[p0:p1, C_ID * 128 + p0:C_ID * 128 + p1], ("cstb", None))
                                for n_, srcv in enumerate([V(bE[p0:p1, q, c64], ("bE", q)), V(kE[p0:p1, q, c64], ("kE", q)),
                                                           V(vb[p0:p1, fl, c64], ("vb", fl))]):
                                    self.tr(V(self.ps[p0:p1, bt_, :].bitcast(BF16)[:, (q * 3 + n_) * 64:(q * 3 + n_ + 1) * 64], ("ps", bt_)),
                                            srcv, idn)
                        self.cp(V(TM[:, par, :, :, :].rearrange("p q a b -> p (q a b)"), TMk), self.psb(bt_, PG * 192), eng="act")

                    def s0():
                        b1, b2, b3 = self.bank("x"), self.bank("x"), self.bank("x")
                        for q in range(PG):
                            for (p0, p1) in HH:
                                rhs_ar = V(AR[p0:p1, 0:nar, q, c64], ("AR", q))
                                self.mm(V(self.ps[p0:p1, b1, q * 128:q * 128 + nar * 64].rearrange("p (a b) -> p a b", b=64), ("ps", b1)),
                                        V(bt[p0:p1, q, c64], ("bt", q)), rhs_ar)
                                self.mm(V(self.ps[p0:p1, b2, q * 128:q * 128 + nar * 64].rearrange("p (a b) -> p a b", b=64), ("ps", b2)),
                                        V(kt_[p0:p1, q, c64], ("kt_", q)), rhs_ar)
                                self.mm(V(self.ps[p0:p1, b3, q * 64:(q + 1) * 64], ("ps", b3)),
                                        V(AR[p0:p1, 0, q, c64], ("AR", q)), V(bt[p0:p1, q, c64], ("bt", q)))
                        if so:
                            self.tt(V(X1m[:, par, :, 0, :], X1k), V(ps3(b1, 128)[:, :, 0:64], ("ps", b1)), M1s, ALU.mult)
                            self.tt(V(X2m[:, par, :, 0, :], X2k), V(ps3(b2, 128)[:, :, 0:64], ("ps", b2)), M1s, ALU.mult)
                        else:
                            self.tt(V(X1m[:, par, :, :, :].rearrange("p q a b -> p q (a b)"), X1k), V(ps3(b1, 128), ("ps", b1)), M1, ALU.mult)
                            self.tt(V(X2m[:, par, :, :, :].rearrange("p q a b -> p q (a b)"), X2k), V(ps3(b2, 128), ("ps", b2)), M1, ALU.mult)
                        self.tt(V(Lb[:, 0, :, :], ("Lb", 0)), V(ps3(b3, 64), ("ps", b3)), SL64, ALU.mult)
                        self.cp(V(ZP[:, 0, :, 0:64], ("ZP", 0)), V(X1m[:, par, :, 0, :], X1k))
                        self.tt(V(ZP[:, 0, :, 64:128], ("ZP", 0)), V(X1m[:, par, :, 0, :], X1k), ID64, ALU.add)

                    def sk(k):
                        cur, nxt = (k - 1) % 2, k % 2
                        ba, bb = self.bank("x"), self.bank("x")
                        for q in range(PG):
                            for (p0, p1) in HH:
                                if k == 1:
                                    rz = V(ZP[p0:p1, cur, q, 0:64], ("ZP", cur))
                                    oz = V(self.ps[p0:p1, ba, q * 128:q * 128 + 64], ("ps", ba))
                                elif k == 5:
                                    rz = V(ZP[p0:p1, cur, q, 64:128], ("ZP", cur))
                                    oz = V(self.ps[p0:p1, ba, q * 128 + 64:q * 128 + 128], ("ps", ba))
                                else:
                                    rz = V(ZP[p0:p1, cur, q, :], ("ZP", cur))
                                    oz = V(self.ps[p0:p1, ba, q * 128:(q + 1) * 128], ("ps", ba))
                                self.mm(oz, V(Lb[p0:p1, cur, q, :], ("Lb", cur)), rz)
                                self.mm(V(self.ps[p0:p1, bb, q * 64:(q + 1) * 64], ("ps", bb)),
                                        V(ZP[p0:p1, cur, q, 0:64], ("ZP", cur)), V(Lb[p0:p1, cur, q, :], ("Lb", cur)))
                        pa3 = ps3(ba, 128)
                        if k < 5:
                            self.cp(V(ZP[:, nxt, :, 0:64], ("ZP", nxt)), V(pa3[:, :, 0:64], ("ps", ba)), eng="act")
                        if k == 1:
                            self.cp(V(ZP[:, nxt, :, 64:128], ("ZP", nxt)), V(ZP[:, cur, :, 64:128], ("ZP", cur)))
                        else:
                            self.tt(V(ZP[:, nxt, :, 64:128], ("ZP", nxt)), V(pa3[:, :, 64:128], ("ps", ba)),
                                    V(ZP[:, cur, :, 64:128], ("ZP", cur)), ALU.add)
                        self.cp(V(Lb[:, nxt, :, :], ("Lb", nxt)), V(ps3(bb, 64), ("ps", bb)), eng="act")

                    def s_fin():
                        cur = 1
                        ba = self.bank("x")
                        for q in range(PG):
                            for (p0, p1) in HH:
                                self.mm(V(self.ps[p0:p1, ba, q * 64:(q + 1) * 64], ("ps", ba)),
                                        V(Lb[p0:p1, cur, q, :], ("Lb", cur)), V(ZP[p0:p1, cur, q, 64:128], ("ZP", cur)))
                        self.tt(V(PTf[:, par, :, :], PTk), V(ps3(ba, 64), ("ps", ba)),
                                V(ZP[:, cur, :, 64:128], ("ZP", cur)), ALU.add)
                    return [s_tr, s0] + [(lambda k=k: sk(k)) for k in range(1, 6)] + [s_fin]

                def chain_stages(j):
                    par = j % 2
                    c64 = slice(j * 64, (j + 1) * 64)
                    X1k, X2k, TMk, PTk = ("X1m", par), ("X2m", par), ("TM", par), ("PTf", par)

                    def cX():
                        bx = self.bank("m")
                        for q in range(PG):
                            fq = f0 + q
                            for (p0, p1) in HH:
                                ox = V(self.ps[p0:p1, bx, q * 64:(q + 1) * 64], ("ps", bx))
                                self.mm(ox, V(AR[p0:p1, 0, q, c64], ("AR", q)), V(self.Tbf[p0:p1, fq, :], ("Tbf", fq)), start=True, stop=False)
                                self.mm(ox, V(X2m[p0:p1, par, q, 0, :], X2k), V(TM[p0:p1, par, q, 2, :], TMk), start=False, stop=True)
                        self.cp(V(Xb[:, :, :], ("Xb", None)), V(ps3(bx, 64), ("ps", bx)), eng="act")

                    def cU():
                        bu = self.bank("m")
                        for q in range(PG):
                            for (p0, p1) in HH:
                                self.mm(V(self.ps[p0:p1, bu, q * 64:(q + 1) * 64], ("ps", bu)), V(PTf[p0:p1, par, q, :], PTk),
                                        V(Xb[p0:p1, q, :], ("Xb", None)))
                        self.cp(V(Ub[:, :, :], ("Ub", None)), V(ps3(bu, 64), ("ps", bu)), eng="act")

                    def cY():
                        if so:
                            return
                        by = self.bank("m")
                        for q in range(PG):
                            fq = f0 + q
                            for (p0, p1) in HH:
                                oy = V(self.ps[p0:p1, by, q * 64:(q + 1) * 64], ("ps", by))
                                self.mm(oy, V(self.Tbf[p0:p1, fq, :], ("Tbf", fq)), V(AR[p0:p1, 1, q, c64], ("AR", q)), start=True, stop=False)
                                self.mm(oy, V(Ub[p0:p1, q, :], ("Ub", None)), V(X1m[p0:p1, par, q, 1, :], X1k), start=False, stop=False)
                                self.mm(oy, V(TM[p0:p1, par, q, 2, :], TMk), V(X2m[p0:p1, par, q, 1, :], X2k), start=False, stop=True)
                        self.cp(V(yr[:, :, c64], ("yr", None)), V(ps3(by, 64), ("ps", by)), eng="act")

                    def cT():
                        bn = self.bank("m")
                        for q in range(PG):
                            for (p0, p1) in HH:
                                on = V(self.ps[p0:p1, bn, q * 64:(q + 1) * 64], ("ps", bn))
                                self.mm(on, V(TM[p0:p1, par, q, 0, :], TMk), V(Ub[p0:p1, q, :], ("Ub", None)), start=True, stop=False)
                                self.mm(on, V(TM[p0:p1, par, q, 1, :], TMk), V(TM[p0:p1, par, q, 2, :], TMk), start=False, stop=True)
                        Tg = V(self.Tst[:, f0:f0 + PG, :], [("Tst", f0 + q) for q in range(PG)])
                        self.tt(Tg, Tg, V(GC[:, :, j:j + 1].to_broadcast([128, PG, 64]), ("GC", None)), ALU.mult)
                        self.tt(Tg, Tg, V(ps3(bn, 64), ("ps", bn)), ALU.add)
                        self.cp(V(self.Tbf[:, f0:f0 + PG, :], [("Tbf", f0 + q) for q in range(PG)]), Tg, eng="act")
                    return [cX, cU, cY, cT]

                NJ = T // 64
                for j in range(NJ + 1):
                    inv = inv_stages(j) if j < NJ else []
                    ch = chain_stages(j - 1) if j > 0 else []
                    ci = 0
                    for s_i, st_fn in enumerate(inv):
                        st_fn()
                        if s_i % 2 == 1 and ci < len(ch):
                            ch[ci]()
                            ci += 1
                    while ci < len(ch):
                        ch[ci]()
                        ci += 1
                if not so:
                    for q in range(PG):
                        f = gq * 4 + sg * PG + q
                        v1 = V(f1[:, 0, :], ("f1", None))
                        v2 = V(f2[:, 0, :], ("f2", None))
                        hv = V(h1[:, 0, :], ("h1", None))
                        yv = V(yr[:, q, :], ("yr", None))
                        self.cp(hv, yv)
                        b = self.bank("x")
                        m_ps = self.psf(b, T)
                        self.mm(m_ps, BLK, hv)
                        self.stt(v1, m_ps, -1.0 / 64, yv, ALU.mult, ALU.add)
                        self.act(hv, v1, AF.Square)
                        b = self.bank("x")
                        v_ps = self.psf(b, T)
                        self.mm(v_ps, BLK, hv)
                        self.act(v2, v_ps, AF.Sqrt, bias=self.ppv(254), scale=1.0 / 64)
                        self.recip(v2, v2)
                        self.tt(v1, v1, v2, ALU.mult)
                        self.ts(v1, v1, self.ppv(self.PP_LNW + f), ALU.mult, self.ppv(self.PP_LNB + f), ALU.add)
                        self.tt(v1, v1, V(bon[:, q, :], ("bon", q)), ALU.add)
                        b = self.bank("x")
                        g_ps = self.psf(b, T)
                        for k2 in range(2):
                            self.mm(g_ps, V(self.lorab[:, 2 + k2, f * 128:(f + 1) * 128], ("lorab", 2)), V(spg[:, k2, :], ("spg", k2)),
                                    start=(k2 == 0), stop=(k2 == 1))
                        self.tt(V(self.ymixT[:, 8 + f, :], ("ymixT", 8 + f)), g_ps, v1, ALU.mult)
        if not so:
            self.debug_dump("ymix_rwkv", V(self.ymixT[:, 8:16, :], ("ymixT", None)))
            self.debug_dump("Tst", V(self.Tst[:, :, :], ("Tst", None)))

    def resid_tm(self, w2d, lhs, k0=0):
        for blk in range(4):
            def ev(c, ps, blk=blk):
                xv = V(self.xres[:, c, blk * 512:(blk + 1) * 512], ("xres", c))
                self.tt(xv, ps, xv, ALU.add)
            self.lin_tm(w2d, blk * 512, 512, lhs, ev, k0=k0)

    def out_proj(self):
        self.resid_tm(self.w_out, lambda kt, c: V(self.ymixT[:, kt, c * 128:(c + 1) * 128], ("ymixT", kt)))

    def xattn(self):
        self.rmsnorm_hT(self.xc, self.ppv(self.PP_G2, 16))
        self.begin_phase()
        o = 0
        qT, o = self.carve(o, "qT", [KT, T], BF16)
        PTt, o = self.carve(o, "PTt", [2, T], BF16)
        pex, o = self.carve(o, "pex", [1, 256], F32)
        pn, o = self.carve(o, "pn", [1, 256], BF16)
        sm3, o = self.carve(o, "sm3", [1, 8], F32)
        oT = self.ymixT
        for blk in range(4):
            def evq(off, width, ps, blk=blk):
                self.cp(V(qT[:, blk * 4 + off // 128, :], ("qT", blk * 4 + off // 128)), ps, eng="act")
            self.lin_fm(self.wq, blk * 512, [(i * 128, 128) for i in range(4)], evq)
        sc = 512 ** -0.5
        for hh in range(4):
            for c in range(NCH):
                b = self.bank("x")
                sp = self.psf(b, 256)
                for d in range(4):
                    self.mm(sp, V(qT[:, hh * 4 + d, c * 128:(c + 1) * 128], ("qT", hh * 4 + d)),
                            V(self.kT[:, hh * 4 + d, :], ("kT", None)), start=(d == 0), stop=(d == 3))
                mx = V(sm3[:, 0, 0:1], ("sm3", 0))
                o1, i1 = mx.ap, sp.ap
                self.S.add("dve", lambda e, o1=o1, i1=i1: e.reduce_max(o1, i1, AX.X), r=[sp], w=[mx])
                nb = V(sm3[:, 0, 1:2], ("sm3", 1))
                self.ts(nb, mx, -sc, ALU.mult)
                rs = V(sm3[:, 0, 2:3], ("sm3", 2))
                pe_ = V(pex[:, 0, :], ("pex", None))
                self.act(pe_, sp, AF.Exp, bias=nb, scale=sc, accum=rs)
                self.recip(rs, rs)
                pnv = V(pn[:, 0, :], ("pn", None))
                self.ts(pnv, pe_, rs, ALU.mult)
                b = self.bank("x")
                for mt in range(2):
                    self.tr(self.psb(b, 128, c0=mt * 128), V(pn[:, 0, mt * 128:(mt + 1) * 128], ("pn", None)), self.identb)
                self.cp(V(PTt[:, :, c * 128:(c + 1) * 128], ("PTt", None)),
                        V(self.psb(b, 256).ap.rearrange("p (a b) -> p a b", b=128), ("ps", b)), eng="act")
            for d in range(4):
                b = self.bank("m")
                op_ = self.psf(b, T)
                for mt in range(2):
                    self.mm(op_, V(self.Vm[:, mt, hh * 512 + d * 128:hh * 512 + (d + 1) * 128], ("Vm", None)),
                            V(PTt[:, mt, :], ("PTt", None)), start=(mt == 0), stop=(mt == 1))
                self.cp(V(oT[:, hh * 4 + d, :], ("ymixT", hh * 4 + d)), op_, eng="act")
        self.resid_tm(self.wo, lambda kt, c: V(oT[:, kt, c * 128:(c + 1) * 128], ("ymixT", kt)))

    def ffn(self):
        self.rmsnorm_hT(self.xc, self.ppv(self.PP_G3, 16))
        self.begin_phase()
        o = 0
        g1T, o = self.carve(o, "g1T", [KT, T], BF16)
        rl, o = self.carve(o, "rl", [2, T], F32)
        for qt in range(4):
            for blk in range(4):
                def ev1(off, width, ps, blk=blk):
                    i = self.rot.get("rl", 0)
                    self.rot["rl"] = i + 1
                    r = V(rl[:, i % 2, :], ("rl", i % 2))
                    self.act(r, ps, AF.Relu)
                    kt = blk * 4 + off // 128
                    self.tt(V(g1T[:, kt, :], ("g1T", kt)), r, r, ALU.mult)
                self.lin_fm(self.w1, qt * 2048 + blk * 512, [(i * 128, 128) for i in range(4)], ev1)
            self.resid_tm(self.w2, lambda kt, c: V(g1T[:, kt, c * 128:(c + 1) * 128], ("g1T", kt)), k0=qt * 2048)

    def convert_some(self):
        if self.conv_skip > 0:
            self.conv_skip -= 1
            return
        for _ in range(self.conv_per_call):
            if self.conv_list:
                self.load_w(*self.conv_list.pop(0))

    def final(self, yout, i, nxt=None):
        self.begin_phase()
        o = 0
        ob, o = self.carve(o, "ob", [2, D], F32)
        gfb, o = self.carve(o, "gfb", [1, D], F32)
        self.gf_bc = V(gfb[:, 0, :], ("gfb", None))
        self.dma(self.gf_bc, self.bc_dram.partition_broadcast(128)[:, 0, 48 + 1024:48 + 1024 + 2048])
        for c in range(NCH):
            x = self.xc(c)
            junk = V(self.junk[:, :], ("hn", None))
            ss = V(self.sm[:, 0:1], ("sm", 0))
            self.act(junk, x, AF.Square, accum=ss)
            sq = V(self.sm[:, 1:2], ("sm", 1))
            self.act(sq, ss, AF.Sqrt, bias=self.eps_v, scale=1.0 / D)
            rstd = V(self.sm[:, 2:3], ("sm", 2))
            self.recip(rstd, sq)
            ov = V(ob[:, c % 2, :], ("ob", c % 2))
            self.stt(ov, x, rstd, self.gf_bc, ALU.mult, ALU.mult)
            self.dma(yout[i * T + c * 128:i * T + (c + 1) * 128, :], ov)
            if nxt is not None:
                self.dma(V(self.xres[:, c, :], ("xres", c)), nxt[0][nxt[1] * T + c * 128:nxt[1] * T + (c + 1) * 128, :])


def host_tables(inp):
    L = 0
    pp = np.zeros((128, 256), np.float32)
    col = lambda v, n: np.asarray(v, np.float32).reshape(n, 128).T
    pp[:, 0:16] = col(inp["norm_mix_g"][L], 16)
    pp[:, 16:32] = col(inp["norm_x_g"][L], 16)
    pp[:, 32:48] = col(inp["norm_ffn_g"][L], 16)
    pp[:, 48:64] = col(inp["norm_mem_g"][L], 16)
    cw = np.asarray(inp["ssd_conv_w"][L], np.float32)[:, 0, :]
    pp[:, 64:112] = cw.T.reshape(12, 128, 4).transpose(1, 0, 2).reshape(128, 48)
    pp[:, 112:124] = col(inp["ssd_conv_b"][L], 12)
    mu = np.asarray(inp["rwkv_mu"][L], np.float32)
    pp[:, 124:148] = col(mu[0:3072], 24)
    pp[0:96, 148] = mu[3072:3168]
    pp[0:96, 149] = mu[3168:3264]
    pp[:, 150:152] = col(mu[3264:3520], 2)
    pp[:, 152:160] = col(inp["rwkv_w0"][L], 8)
    pp[:, 160:168] = col(inp["rwkv_a0"][L], 8)
    pp[:, 168:176] = col(inp["rwkv_k_k"][L], 8)
    ka = np.asarray(inp["rwkv_k_a"][L], np.float32)
    pp[:, 176:184] = col(ka, 8)
    pp[:, 192:200] = col(np.asarray(inp["rwkv_r_k"][L], np.float32).reshape(-1), 8)
    pp[:, 200:208] = col(inp["rwkv_ln_w"][L], 8)
    pp[:, 208:216] = col(inp["rwkv_ln_b"][L], 8)
    pp[:, 253] = 1.0
    pp[:, 254] = LN_EPS
    pp[:, 255] = EPS
    bc = np.concatenate([
        np.asarray(inp["ssd_a_log"][L], np.float32), np.asarray(inp["ssd_dt_bias"][L], np.float32),
        np.asarray(inp["ssd_d"][L], np.float32), np.asarray(inp["ssd_norm_g"][L], np.float32),
        np.asarray(inp["final_norm_g"], np.float32)])[None, :]
    lora = np.concatenate([np.asarray(inp["rwkv_w2"][L], np.float32), np.asarray(inp["rwkv_a2"][L], np.float32),
                           np.asarray(inp["rwkv_g2"][L], np.float32)], axis=0)
    return pp, np.ascontiguousarray(bc), np.ascontiguousarray(lora)


def core_inputs(inp, xo, xp, mem_b, flagv, tabs, cst):
    pp, bc, lora = tabs
    f32 = lambda a: np.ascontiguousarray(np.asarray(a, np.float32))
    return {
        "xo": f32(xo), "xp": f32(xp), "flag": np.full((128, 1), flagv, np.float32), "mem": f32(mem_b), "cst": cst,
        "w_in": f32(inp["w_in"][0]), "w_out": f32(inp["w_out"][0]), "wq": f32(inp["xattn_wq"][0]),
        "wk": f32(inp["xattn_wk"][0]), "wv": f32(inp["xattn_wv"][0]), "wo": f32(inp["xattn_wo"][0]),
        "w1": f32(inp["ffn_w1"][0]), "w2": f32(inp["ffn_w2"][0]), "pp": pp, "bc": bc, "lora": lora,
    }


def kernel(**inp):
    x = np.asarray(inp["x"], np.float32)
    mem = np.asarray(inp["mem"], np.float32)
    Bn, Sq, _ = x.shape
    half = Sq // 2
    nt = half // T
    bld = Builder(nt, nt)
    nc = bld.build()
    tabs = host_tables(inp)
    cst = make_consts()
    in_maps = []
    for core in range(8):
        b, h = core // 2, core % 2
        in_maps.append(core_inputs(inp, x[b, h * half:(h + 1) * half], x[b, 0:half], mem[b], float(h), tabs, cst))
    res = run_bass_kernel_spmd(nc, in_maps, core_ids=list(range(8)))
    out = np.zeros((Bn, Sq, D), np.float32)
    for core in range(8):
        b, h = core // 2, core % 2
        out[b, h * half:(h + 1) * half] = np.asarray(res.results[core]["y"], np.float32)
    return out
```

```python
import numpy as np
import concourse.bass as bass
import concourse.mybir as mybir
from concourse.bass_utils import run_bass_kernel_spmd

F32 = mybir.dt.float32
BF16 = mybir.dt.bfloat16
AF = mybir.ActivationFunctionType
ALU = mybir.AluOpType
AX = mybir.AxisListType

ENGS = ["pe", "act", "dve", "pool", "sp"]


class V:
    __slots__ = ("ap", "keys")

    def __init__(self, ap, keys):
        self.ap = ap
        self.keys = [keys] if isinstance(keys, tuple) else keys


class Op:
    __slots__ = ("eng", "idx", "fn", "deps", "needed", "signum", "isdma", "slot", "dval", "epoch")

    def __init__(self, eng, idx, fn, isdma):
        self.eng, self.idx, self.fn, self.isdma = eng, idx, fn, isdma
        self.epoch = 0
        self.deps = {}
        self.needed = False
        self.signum = 0
        self.slot = None
        self.dval = 0


class St:
    __slots__ = ("writer", "readers")

    def __init__(self):
        self.writer = None
        self.readers = []


class Sched:
    def __init__(self):
        self.ops = {e: [] for e in ENGS}
        self.state = {}
        self.ndma = {"pool": 4, "sp": 8}
        self.dma_count = {q: 0 for q in self.ndma}
        self.epoch = 0

    def new_epoch(self):
        self.epoch += 1

    def _dep(self, op, d):
        if d is None or d is op:
            return
        if d.eng == "pe" and op.eng == "pe" and not d.isdma and not op.isdma:
            return
        k = ("dma", d.eng, d.slot) if d.isdma else d.eng
        cur = op.deps.get(k)
        if cur is None or (d.dval if d.isdma else d.idx) > (cur.dval if cur.isdma else cur.idx):
            op.deps[k] = d

    def _touch(self, op, key, is_write):
        name, sub = key
        ent = self.state.setdefault(name, {})
        if sub is None:
            targets = list(ent.keys())
        else:
            targets = [k for k in (sub, None) if k in ent]
        for k in targets:
            st = ent[k]
            self._dep(op, st.writer)
            if is_write:
                for r in st.readers:
                    self._dep(op, r)
        if sub is None and is_write:
            ent.clear()
        st = ent.get(sub)
        if st is None:
            st = ent[sub] = St()
        if is_write:
            st.writer = op
            st.readers = []
        else:
            st.readers.append(op)

    def add(self, eng, fn, r=(), w=(), dma=False):
        lst = self.ops[eng]
        op = Op(eng, len(lst), fn, dma)
        op.epoch = self.epoch
        if dma:
            n = self.dma_count[eng]
            self.dma_count[eng] = n + 1
            op.slot = n % self.ndma[eng]
            op.dval = 16 * (n // self.ndma[eng] + 1)
        for v in r:
            for k in (v.keys if isinstance(v, V) else [v]):
                self._touch(op, k, False)
        for v in w:
            for k in (v.keys if isinstance(v, V) else [v]):
                self._touch(op, k, True)
        lst.append(op)
        return op

    def finalize(self):
        CE = ["pe", "act", "dve", "pool"]
        for e in ENGS:
            for op in self.ops[e]:
                for k, d in list(op.deps.items()):
                    if (not d.isdma) and d.epoch != op.epoch:
                        del op.deps[k]
                    else:
                        d.needed = True
        self.final = [dict() for _ in range(self.epoch + 1)]
        for e in CE:
            last = {}
            for op in self.ops[e]:
                if not op.isdma:
                    last[op.epoch] = op
            for op in last.values():
                op.needed = True
            n, ep = 0, -1
            for op in self.ops[e]:
                if op.epoch != ep:
                    n, ep = 0, op.epoch
                if op.needed and not op.isdma:
                    n += 1
                    op.signum = n
                    self.final[ep][e] = n

    def emit(self, eng, e, sems, dsems):
        known = {}
        last_on_slot = {}
        cur_ep = 0

        def barrier(b):
            for f, n in self.final[b].items():
                if known.get(f, 0) < n:
                    e.wait_ge(sems[f][b % 4], n)
            if b >= 2 and eng in sems:
                e.sem_clear(sems[eng][(b - 2) % 4])

        for op in self.ops[eng]:
            while cur_ep < op.epoch:
                barrier(cur_ep)
                cur_ep += 1
                known = {k: v for k, v in known.items() if isinstance(k, tuple)}
            waits = []
            for k, d in op.deps.items():
                if d.isdma:
                    waits.append((k, dsems[d.eng][d.slot], d.dval))
                else:
                    waits.append((k, sems[d.eng][op.epoch % 4], d.signum))
            if op.isdma:
                prev = last_on_slot.get(op.slot)
                if prev is not None:
                    waits.append((("dma", eng, op.slot), dsems[eng][op.slot], prev.dval))
                last_on_slot[op.slot] = op
            for k, sem, val in waits:
                if known.get(k, 0) >= val:
                    continue
                e.wait_ge(sem, val)
                known[k] = val
            ins = op.fn(e)
            if op.isdma:
                ins.then_inc(dsems[eng][op.slot], 16)
            elif op.needed:
                ins.then_inc(sems[eng][op.epoch % 4], 1)
        while cur_ep <= self.epoch:
            barrier(cur_ep)
            cur_ep += 1
        if eng in dsems:
            for slot, op in last_on_slot.items():
                e.wait_ge(dsems[eng][slot], op.dval)


D = 2048
KT = 16
T = 512
NCH = 4
D_IN = 6096
Z0, XS0, B0, C0, DT0 = 0, 1024, 2048, 2304, 2560
R0, K0, V0, PW0, PA0, PG0 = 2576, 3600, 4624, 5648, 5744, 5840
EPS = 1e-6
LN_EPS = 64e-5
ESQ = float(np.exp(-0.5))

C_ID, C_UT, C_SL, C_ONE, C_BLK, C_M1, C_SL64, C_ID64, C_RST = range(9)
NCST = 13


def make_consts():
    p = np.arange(128)[:, None]
    f = np.arange(128)[None, :]
    c = np.zeros((128, NCST * 128), np.float32)
    c[:, C_ID * 128:(C_ID + 1) * 128] = (p == f)
    c[:, C_UT * 128:(C_UT + 1) * 128] = (p <= f)
    c[:, C_SL * 128:(C_SL + 1) * 128] = (p > f)
    c[:, C_ONE * 128:(C_ONE + 1) * 128] = 1.0
    c[:, C_BLK * 128:(C_BLK + 1) * 128] = ((p // 64) == (f // 64))
    s = p % 64
    t = np.arange(64)[None, :]
    c[:, C_M1 * 128:C_M1 * 128 + 64] = (s < t)
    c[:, C_M1 * 128 + 64:(C_M1 + 1) * 128] = (s <= t)
    c[:, C_SL64 * 128:C_SL64 * 128 + 64] = (s > t)
    c[:, C_ID64 * 128:C_ID64 * 128 + 64] = (s == t)
    rst = np.ones((128, 512), np.float32)
    rst[:, ::64] = 0.0
    c[:, C_RST * 128:C_RST * 128 + 512] = rst
    return c


class Builder:
    def __init__(self, n_pre, n_own, dbg=None):
        self.n_pre, self.n_own = n_pre, n_own
        self.dbg = dbg or []
        self.S = Sched()
        self.nc = bass.Bass("TRN2", target_bir_lowering=False)
        self.rot = {}
        self.dram = {}
        self.sb = {}

    def din(self, name, shape, dtype=F32):
        t = self.nc.dram_tensor(name, list(shape), dtype, kind="ExternalInput").ap()
        self.dram[name] = t
        return t

    def dout(self, name, shape, dtype=F32):
        t = self.nc.dram_tensor(name, list(shape), dtype, kind="ExternalOutput").ap()
        self.dram[name] = t
        return t

    def alloc(self, stack, name, shape, dtype):
        t = stack.enter_context(self.nc.sbuf_tensor(name, list(shape), dtype))
        self.sb[name] = t
        return t

    def bank(self, grp):
        lo, n = {"m": (0, 4), "x": (4, 4)}[grp]
        i = self.rot.get(grp, 0)
        self.rot[grp] = i + 1
        return lo + i % n

    def psf(self, b, n=512, p0=0, p1=128, c0=0):
        return V(self.ps[p0:p1, b, c0:c0 + n], ("ps", b))

    def psb(self, b, n=1024, p0=0, p1=128, c0=0):
        return V(self.ps[p0:p1, b, :].bitcast(BF16)[:, c0:c0 + n], ("ps", b))

    def mm(self, out, lhsT, rhs, start=True, stop=True):
        o, l, r = out.ap, lhsT.ap, rhs.ap
        self.S.add("pe", lambda e: e.matmul(o, l, r, start=start, stop=stop), r=[lhsT, rhs], w=[out])

    def tr(self, out, in_, ident):
        o, i, d = out.ap, in_.ap, ident.ap
        self.S.add("pe", lambda e: e.transpose(o, i, d), r=[in_, ident], w=[out])

    def act(self, out, in_, func, bias=None, scale=None, accum=None):
        o, i = out.ap, in_.ap
        rs = [in_]
        ws = [out]
        kw = {}
        if bias is not None:
            if isinstance(bias, V):
                rs.append(bias)
                kw["bias"] = bias.ap
            else:
                kw["bias"] = float(bias)
        if scale is not None:
            if isinstance(scale, V):
                rs.append(scale)
                kw["scale"] = scale.ap
            else:
                kw["scale"] = float(scale)
        if accum is not None:
            ws.append(accum)
            kw["accum_out"] = accum.ap
        self.S.add("act", lambda e: e.activation(o, i, func, **kw), r=rs, w=ws)

    def tt(self, out, in0, in1, op, eng="dve"):
        o, a, b = out.ap, in0.ap, in1.ap
        self.S.add(eng, lambda e: e.tensor_tensor(o, a, b, op), r=[in0, in1], w=[out])

    def ts(self, out, in0, s1, op0, s2=None, op1=None, eng="dve"):
        o, a = out.ap, in0.ap
        rs = [in0]
        a1 = s1
        if isinstance(s1, V):
            rs.append(s1)
            a1 = s1.ap
        a2 = s2
        if isinstance(s2, V):
            rs.append(s2)
            a2 = s2.ap
        if op1 is None:
            self.S.add(eng, lambda e: e.tensor_scalar(o, a, a1, None, op0), r=rs, w=[out])
        else:
            self.S.add(eng, lambda e: e.tensor_scalar(o, a, a1, a2, op0, op1), r=rs, w=[out])

    def stt(self, out, in0, sc, in1, op0, op1):
        o, a, b = out.ap, in0.ap, in1.ap
        rs = [in0, in1]
        s = sc
        if isinstance(sc, V):
            rs.append(sc)
            s = sc.ap
        self.S.add("dve", lambda e: e.scalar_tensor_tensor(o, a, s, b, op0, op1), r=rs, w=[out])

    def cp(self, out, in_, eng="dve"):
        o, i = out.ap, in_.ap
        if eng == "act":
            self.S.add("act", lambda e: e.copy(o, i), r=[in_], w=[out])
        else:
            self.S.add(eng, lambda e: e.tensor_copy(o, i), r=[in_], w=[out])

    def recip(self, out, in_):
        o, i = out.ap, in_.ap
        self.S.add("dve", lambda e: e.reciprocal(o, i), r=[in_], w=[out])

    def memset(self, out, val, eng="dve"):
        o = out.ap
        self.S.add(eng, lambda e: e.memset(o, val), w=[out])

    def dma(self, out, in_, q="sp", **kw):
        o = out.ap if isinstance(out, V) else out
        i = in_.ap if isinstance(in_, V) else in_
        rs = [in_] if isinstance(in_, V) else []
        ws = [out] if isinstance(out, V) else []
        return self.S.add(q, lambda e: e.dma_start(out=o, in_=i, **kw), r=rs, w=ws, dma=True)

    def load_w(self, w2d, k0, c0, ncols):
        i = self.rot.get("w", 0)
        self.rot["w"] = i + 1
        slot = i % 2
        dst = V(self.wbuf[:, slot, :, 0:ncols], ("wbuf", slot))
        key = (id(w2d), k0, c0, ncols)
        use = self.use_scratch and ncols >= 128
        if use and key in self.wscr:
            off = self.wscr[key]
            src = self.scr[:, off:off + KT * ncols].rearrange("p (kt c) -> p kt c", c=ncols)
            self.dma(dst, V(src, ("scr", key)), q="sp")
            return slot
        src = w2d[k0:k0 + 2048, c0:c0 + ncols].rearrange("(kt p) c -> p kt c", p=128)
        self.dma(dst, src, q="pool")
        if use:
            off = self.scr_off
            self.scr_off += KT * ncols
            assert self.scr_off <= self.SCR_N
            self.wscr[key] = off
            self.dma(V(self.scr[:, off:off + KT * ncols].rearrange("p (kt c) -> p kt c", c=ncols), ("scr", key)), dst, q="sp")
        return slot

    def wv(self, slot, kt, c0, n, rows=128):
        return V(self.wbuf[0:rows, slot, kt, c0:c0 + n], ("wbuf", slot))

    def hTv(self, kt, c0=0, n=T):
        return V(self.hT[:, kt, c0:c0 + n], ("hT", kt))

    def lin_fm(self, w2d, c0, tiles, evac, k0=0, rhs=None):
        ncols = max(o + w for o, w in tiles)
        slot = self.load_w(w2d, k0, c0, ncols)
        rhs = rhs or self.hTv
        for (off, width) in tiles:
            b = self.bank("m")
            out = self.psf(b, T, 0, width)
            for kt in range(KT):
                self.mm(out, self.wv(slot, kt, off, width), rhs(kt), start=(kt == 0), stop=(kt == KT - 1))
            evac(off, width, out)

    def lin_tm(self, w2d, c0, ncols, lhs, evac, k0=0):
        slot = self.load_w(w2d, k0, c0, ncols)
        for c in range(NCH):
            b = self.bank("m")
            out = self.psf(b, ncols)
            for kt in range(KT):
                self.mm(out, lhs(kt, c), self.wv(slot, kt, 0, ncols), start=(kt == 0), stop=(kt == KT - 1))
            evac(c, out)

    def rmsnorm_hT(self, src_c, gpp, nch=NCH, dst=None, width=D):
        nkt = width // 128
        for c in range(nch):
            x = src_c(c)
            junk = V(self.junk[:, 0:width], ("hn", None))
            ss = V(self.sm[:, 0:1], ("sm", 0))
            self.act(junk, x, AF.Square, accum=ss)
            sq = V(self.sm[:, 1:2], ("sm", 1))
            self.act(sq, ss, AF.Sqrt, bias=self.eps_v, scale=1.0 / width)
            rstd = V(self.sm[:, 2:3], ("sm", 2))
            self.recip(rstd, sq)
            hn = V(self.hn[:, 0:width], ("hn", None))
            self.act(hn, x, AF.Copy, scale=rstd)
            for half in range(nkt // 8):
                b = self.bank("x")
                for j in range(8):
                    kt = half * 8 + j
                    self.tr(self.psb(b, 128, c0=j * 128), V(self.hn[:, kt * 128:(kt + 1) * 128], ("hn", None)), self.identb)
                g = V(gpp.ap[:, half * 8:half * 8 + 8].unsqueeze(2).to_broadcast([128, 8, 128]), gpp.keys)
                if dst is None:
                    o = V(self.hT[:, half * 8:half * 8 + 8, c * 128:(c + 1) * 128], [("hT", half * 8 + j) for j in range(8)])
                else:
                    o = dst(half, c)
                pin = V(self.psb(b).ap.rearrange("p (a b) -> p a b", b=128), ("ps", b))
                self.tt(o, pin, g, ALU.mult)

    def build(self):
        import contextlib
        nc = self.nc
        n_pre, n_own = self.n_pre, self.n_own
        xo = self.din("xo", [n_own * T, D])
        xp = self.din("xp", [max(n_pre, 1) * T, D])
        flag = self.din("flag", [128, 1])
        mem = self.din("mem", [256, D])
        cst = self.din("cst", [128, NCST * 128])
        w_in = self.din("w_in", [D, D_IN])
        w_out = self.din("w_out", [D, D])
        wq = self.din("wq", [D, D])
        wk = self.din("wk", [D, D])
        wvv = self.din("wv", [D, D])
        wo = self.din("wo", [D, D])
        w1 = self.din("w1", [D, 4 * D])
        w2 = self.din("w2", [4 * D, D])
        pp = self.din("pp", [128, 256])
        bc = self.din("bc", [1, 16 * 3 + 1024 + 2048])
        lora = self.din("lora", [96 * 2 + 256, 1024])
        yout = self.dout("y", [n_own * T, D])
        self.SCR_N = (D * D_IN + 3 * D * D + 8 * D * D) // 128 + 16 * 4096
        self.scr = nc.dram_tensor("wscr", [128, self.SCR_N], BF16).ap()
        self.scr_off = 0
        self.wscr = {}
        self.use_scratch = False
        dbg_out = {}
        for name, shape, dt_ in self.dbg:
            dbg_out[name] = self.dout("dbg_" + name, shape, dt_)

        with contextlib.ExitStack() as st:
            A = lambda name, shape, dt: self.alloc(st, name, shape, dt)
            self.ps = st.enter_context(nc.psum_tensor("ps", [128, 8, 512], F32))
            self.xres = A("xres", [128, NCH, D], F32)
            self.hT = A("hT", [128, KT, T], BF16)
            self.wbuf = A("wbuf", [128, 2, KT, 512], BF16)
            self.cstb = A("cstb", [128, NCST * 128], BF16)
            self.ppt = A("ppt", [128, 256], F32)
            self.bct = A("bct", [128, 48 + 1024], F32)
            self.lorab = A("lorab", [128, 4, 1024], BF16)
            self.kT = A("kT", [128, KT, 256], BF16)
            self.Vm = A("Vm", [128, 2, D], BF16)
            self.Sst = A("Sst", [128, 1024], F32)
            self.Sbf = A("Sbf", [128, 1024], BF16)
            self.Tst = A("Tst", [128, 8, 64], F32)
            self.Tbf = A("Tbf", [128, 8, 64], BF16)
            self.convh = A("convh", [128, 12, 3], F32)
            self.shifth = A("shifth", [128, 28], F32)
            self.sm = A("sm", [128, 64], F32)
            self.flagt = A("flagt", [128, 1], F32)
            self.hn = A("hn", [128, D], BF16)
            self.junk = self.hn
            self.ymixT = A("ymixT", [128, KT, T], BF16)
            self.ARENA = 16832
            self.arena = A("arena", [128, self.ARENA], F32)

            def cstv(blk, n=128, rows=128, off=0):
                return V(self.cstb[0:rows, blk * 128 + off:blk * 128 + off + n], ("cstb", None))
            self.identb = cstv(C_ID)
            self.eps_v = V(self.ppt[:, 255:256], ("ppt", None))

            sems = {}
            dsems = {}
            for e in ["pe", "act", "dve", "pool"]:
                sems[e] = [st.enter_context(nc.semaphore("s_%s%d" % (e, i))) for i in range(4)]
            for q, n in self.S.ndma.items():
                dsems[q] = [st.enter_context(nc.semaphore("d_%s%d" % (q, i))) for i in range(n)]

            self.cstv = cstv
            self.emit_program(xo, xp, flag, mem, cst, w_in, w_out, wq, wk, wvv, wo, w1, w2, pp, bc, lora, yout, dbg_out)
            self.S.finalize()

            with nc.Block() as block:
                @block.tensor
                def _(e):
                    self.S.emit("pe", e, sems, dsems)

                @block.scalar
                def _(e):
                    self.S.emit("act", e, sems, dsems)

                @block.vector
                def _(e):
                    self.S.emit("dve", e, sems, dsems)

                @block.gpsimd
                def _(e):
                    self.S.emit("pool", e, sems, dsems)

                @block.sync
                def _(e):
                    self.S.emit("sp", e, sems, dsems)
        return nc

    def carve4(self, off, name, shape, dtype):
        a, o = self.carve(off, name, [shape[0] * shape[1], shape[2], shape[3]], dtype)
        return a.rearrange("p (u q) a b -> p u q a b", u=shape[0]), o

    def begin_phase(self):
        pend = {}
        for name in getattr(self, "arena_names", set()):
            ent = self.S.state.pop(name, {})
            for st_ in ent.values():
                for d in ([st_.writer] if st_.writer is not None else []) + st_.readers:
                    k = ("dma", d.eng, d.slot) if d.isdma else d.eng
                    cur = pend.get(k)
                    if cur is None or (d.dval if d.isdma else d.idx) > (cur.dval if cur.isdma else cur.idx):
                        pend[k] = d
        for d in getattr(self, "pending", []):
            k = ("dma", d.eng, d.slot) if d.isdma else d.eng
            cur = pend.get(k)
            if cur is None or (d.dval if d.isdma else d.idx) > (cur.dval if cur.isdma else cur.idx):
                pend[k] = d
        self.pending = list(pend.values())
        self.arena_names = set()

    def carve(self, off, name, shape, dtype, at=None):
        if name not in self.arena_names:
            self.arena_names.add(name)
            st_ = St()
            st_.readers = list(self.pending)
            self.S.state[name] = {None: st_}
        n = int(np.prod(shape))
        if at is not None:
            words = n if dtype == F32 else (n + 1) // 2
            a = self.arena[:, at:at + words]
            if dtype != F32:
                a = a.bitcast(BF16)[:, 0:n]
            if len(shape) == 2:
                a = a.rearrange("p (a b) -> p a b", b=shape[1])
            elif len(shape) == 3:
                a = a.rearrange("p (a b c) -> p a b c", b=shape[1], c=shape[2])
            return a, off
        words = n if dtype == F32 else (n + 1) // 2
        a = self.arena[:, off:off + words]
        if dtype != F32:
            a = a.bitcast(BF16)[:, 0:n]
        if len(shape) == 2:
            a = a.rearrange("p (a b) -> p a b", b=shape[1])
        elif len(shape) == 3:
            a = a.rearrange("p (a b c) -> p a b c", b=shape[1], c=shape[2])
        assert off + words <= self.ARENA, (name, off + words)
        return a, off + words

    def emit_program(self, xo, xp, flag, mem, cst, w_in, w_out, wq, wk, wvv, wo, w1, w2, pp, bc, lora, yout, dbg_out):
        S = self.S
        self.dbg_out = dbg_out
        self.dma(V(self.cstb[:, :], ("cstb", None)), cst, q="pool")
        self.dma(V(self.ppt[:, :], ("ppt", None)), pp)
        self.dma(V(self.bct[:, :], ("bct", None)), bc.partition_broadcast(128)[:, 0, 0:48 + 1024])
        self.bc_dram = bc
        self.dma(V(self.flagt[:, :], ("flagt", None)), flag)
        self.dma(V(self.lorab[0:96, 0, :], ("lorab", 0)), lora[0:96, :], q="pool")
        self.dma(V(self.lorab[0:96, 1, :], ("lorab", 1)), lora[96:192, :], q="pool")
        self.dma(V(self.lorab[:, 2:4, :], ("lorab", 2)), lora[192:448, :].rearrange("(k p) c -> p k c", p=128), q="pool")
        alog = V(self.bct[:, 0:16], ("bct", "a"))
        self.act(alog, V(self.bct[:, 0:16], ("bct", None)), AF.Exp)
        self.ts(alog, alog, -1.0, ALU.mult)
        self.a_bc = alog
        self.dtb_bc = V(self.bct[:, 16:32], ("bct", None))
        self.D_bc = V(self.bct[:, 32:48], ("bct", None))
        self.ng_bc = V(self.bct[:, 48:48 + 1024], ("bct", None))
        omka = V(self.ppt[:, self.PP_OMKA:self.PP_OMKA + 8], ("ppt", "omka"))
        self.ts(omka, V(self.ppt[:, self.PP_KA:self.PP_KA + 8], ("ppt", None)), -1.0, ALU.mult, 1.0, ALU.add)
        for t, k in [(self.Sst, "Sst"), (self.Sbf, "Sbf"), (self.Tst, "Tst"), (self.Tbf, "Tbf"), (self.convh, "convh"), (self.shifth, "shifth")]:
            self.memset(V(t[:], (k, None)), 0.0)

        self.w_in, self.w_out, self.wq, self.wo, self.w1, self.w2 = w_in, w_out, wq, wo, w1, w2
        self.mem_kv(mem, wk, wvv)
        self.use_scratch = True
        tiles = [(xp, i, True) for i in range(self.n_pre)] + [(xo, i, False) for i in range(self.n_own)]
        self.conv_list = []
        if self.n_pre > 0:
            for wmat in (self.w_out, self.wq, self.wo):
                self.conv_list += [(wmat, 0, blk * 512, 512) for blk in range(4)]
            for qt in range(4):
                self.conv_list += [(self.w1, 0, qt * 2048 + blk * 512, 512) for blk in range(4)]
                self.conv_list += [(self.w2, qt * 2048, blk * 512, 512) for blk in range(4)]
        self.conv_per_call = (len(self.conv_list) + 2 * max(self.n_pre, 1) - 1) // (2 * max(self.n_pre, 1))
        self.load_x(*tiles[0][:2])
        for ti, (xsrc, i, so_) in enumerate(tiles):
            self.S.new_epoch()
            if so_:
                self.mixer(state_only=True, last=(i == self.n_pre - 1))
                if i == self.n_pre - 1:
                    self.apply_flag()
                if ti + 1 < len(tiles):
                    self.load_x(*tiles[ti + 1][:2])
            else:
                self.mixer(state_only=False)
                self.out_proj()
                self.xattn()
                self.ffn()
                self.final(yout, i, tiles[ti + 1][:2] if ti + 1 < len(tiles) else None)

    def ppv(self, c0, n=1):
        return V(self.ppt[:, c0:c0 + n], ("ppt", None))

    PP_G1, PP_G2, PP_G3, PP_GM = 0, 16, 32, 48
    PP_CW, PP_CB, PP_MU = 64, 112, 124
    PP_W0, PP_A0, PP_KK, PP_KA, PP_OMKA, PP_RK, PP_LNW, PP_LNB = 152, 160, 168, 176, 184, 192, 200, 208

    def load_x(self, xsrc, i):
        for c in range(NCH):
            self.dma(V(self.xres[:, c, :], ("xres", c)), xsrc[i * T + c * 128:i * T + (c + 1) * 128, :])

    def xc(self, c):
        return V(self.xres[:, c, :], ("xres", c))

    def apply_flag(self):
        fl = V(self.flagt[:, 0:1], ("flagt", None))
        for t, k in [(self.Sst, "Sst"), (self.Tst, "Tst"), (self.convh, "convh"), (self.shifth, "shifth")]:
            v = V(t[:], (k, None))
            self.ts(v, v, fl, ALU.mult)
        self.cp(V(self.Sbf[:], ("Sbf", None)), V(self.Sst[:], ("Sst", None)), eng="act")
        self.cp(V(self.Tbf[:], ("Tbf", None)), V(self.Tst[:], ("Tst", None)), eng="act")

    def mem_kv(self, mem, wk, wvv):
        self.begin_phase()
        mt, o = self.carve(0, "memt", [2, D], F32)
        for c in range(2):
            self.dma(V(mt[:, c, :], ("memt", c)), mem[c * 128:(c + 1) * 128, :])
        mT, o = self.carve(o, "mT", [KT, 256], BF16)

        def dst(half, c):
            return V(mT[:, half * 8:half * 8 + 8, c * 128:(c + 1) * 128], ("mT", None))
        self.rmsnorm_hT(lambda c: V(mt[:, c, :], ("memt", c)), self.ppv(self.PP_GM, 16), nch=2, dst=dst)
        rhs = lambda kt: V(mT[:, kt, :], ("mT", None))
        for blk in range(4):
            slot = self.load_w(wk, 0, blk * 512, 512)
            for ct in range(4):
                b = self.bank("m")
                out = self.psf(b, 256)
                for kt in range(KT):
                    self.mm(out, self.wv(slot, kt, ct * 128, 128), rhs(kt), start=(kt == 0), stop=(kt == KT - 1))
                self.cp(V(self.kT[:, blk * 4 + ct, :], ("kT", None)), out, eng="act")
        for blk in range(4):
            slot = self.load_w(wvv, 0, blk * 512, 512)
            for c in range(2):
                b = self.bank("m")
                out = self.psf(b, 512)
                for kt in range(KT):
                    self.mm(out, V(mT[:, kt, c * 128:(c + 1) * 128], ("mT", None)), self.wv(slot, kt, 0, 512),
                            start=(kt == 0), stop=(kt == KT - 1))
                self.cp(V(self.Vm[:, c, blk * 512:(blk + 1) * 512], ("Vm", None)), out, eng="act")

    def mixer(self, state_only, last=False):
        self.rmsnorm_hT(self.xc, self.ppv(self.PP_G1, 16))
        self.halo_all = (not state_only) or last
        self.ssd(state_only)
        self.rwkv(state_only)

    def debug_dump(self, name, v):
        if name in self.dbg_out:
            self.dma(self.dbg_out[name], v)

    def ssd(self, so):
        self.begin_phase()
        o = 0
        xbc, o = self.carve(o, "xbc", [12, T], BF16)
        sz, o = self.carve(o, "sz", [NCH, 1024], BF16)
        o_pre = o
        pre, o = self.carve(o, "pre", [2, 3 + T], F32)
        o_acc = o
        acc, o = self.carve(o, "acc", [2, T], F32)
        dtr, o = self.carve(o, "dtr", [NCH, 16], F32)
        dah, o = self.carve(o, "dah", [NCH, 16], BF16)
        dal, o = self.carve(o, "dal", [NCH, 16], BF16)
        xs_tok, o = self.carve(o, "xs_tok", [2, 1024], BF16)
        B_tok, o = self.carve(o, "B_tok", [2, 256], BF16)
        o_rch = o
        rch, o = self.carve(o, "rch", [8, 128], BF16)
        rcl, o = self.carve(o, "rcl", [8, 128], BF16)
        o_LT = o
        LT, o = self.carve(o, "LT", [8, 128], F32)
        MT, o = self.carve(o, "MT", [32, 128], BF16)
        cbm, o = self.carve(o, "cbm", [2, 128], F32)
        xdt0, o = self.carve(o, "xd", [16, 64], BF16, at=o_pre)
        xdd0, o = self.carve(o, "xd", [16, 64], BF16, at=o_pre + 512)
        xdt1, o = self.carve(o, "xdt1", [16, 64], BF16)
        xdd1, o = self.carve(o, "xdd1", [16, 64], BF16)
        xdt, xdd = [xdt0, xdt1], [xdd0, xdd1]
        yy, o = self.carve(o, "yy", [16, 64], F32, at=o_acc)
        tmp, o = self.carve(o, "tmp", [16, 64], F32)
        ysn, o = self.carve(o, "ysn", [1, 1024], BF16)
        sm2, o = self.carve(o, "sm2", [2, 128], F32)
        w_in = self.w_in
        UT, SL, ONE = self.cstv(C_UT), self.cstv(C_SL), self.cstv(C_ONE)

        slot = self.load_w(w_in, 0, DT0, 16)
        for c in range(NCH):
            b = self.bank("x")
            out = self.psf(b, 16)
            for kt in range(KT):
                self.mm(out, self.hTv(kt, c * 128, 128), self.wv(slot, kt, 0, 16), start=(kt == 0), stop=(kt == KT - 1))
            d = V(dtr[:, c, :], ("dtr", c))
            self.tt(d, out, self.dtb_bc, ALU.add)
            self.act(d, d, AF.Exp)
            self.act(d, d, AF.Ln, bias=1.0)
        dall = V(dtr[:, :, :], ("dtr", None))
        da, _ = self.carve(o, "da", [NCH, 16], F32)
        o2 = o + NCH * 16
        dav = V(da[:, :, :], ("da", None))
        self.tt(dav, dall, V(self.a_bc.ap.unsqueeze(1).to_broadcast([128, NCH, 16]), self.a_bc.keys), ALU.mult)
        dahv = V(dah[:, :, :], ("dah", None))
        dalv = V(dal[:, :, :], ("dal", None))
        self.cp(dahv, dav)
        self.tt(dalv, dav, dahv, ALU.subtract)

        def evac_xbc(base_ft):
            def f(off, width, ps):
                ft = base_ft + off // 128
                i = self.rot.get("pre", 0)
                self.rot["pre"] = i + 1
                pb = i % 2
                pv = lambda a, n: V(pre[:, pb, a:a + n], ("pre", pb))
                hal = V(self.convh[:, ft, :], ("convh", ft))
                self.cp(pv(0, 3), hal)
                self.cp(pv(3, T), ps, eng="act")
                self.cp(hal, pv(T, 3))
                av = V(acc[:, pb, :], ("acc", pb))
                cw = lambda k: self.ppv(self.PP_CW + ft * 4 + k)
                self.ts(av, pv(0, T), cw(0), ALU.mult)
                for k in range(1, 4):
                    self.stt(av, pv(k, T), cw(k), av, ALU.mult, ALU.add)
                self.act(V(xbc[:, ft, :], ("xbc", ft)), av, AF.Silu, bias=self.ppv(self.PP_CB + ft))
            return f
        self.lin_fm(w_in, XS0, [(i * 128, 128) for i in range(4)], evac_xbc(0))
        self.lin_fm(w_in, XS0 + 512, [(i * 128, 128) for i in range(4)], evac_xbc(4))
        self.lin_fm(w_in, B0, [(i * 128, 128) for i in range(4 if self.halo_all else 2)], evac_xbc(8))

        if not so:
            for blk in range(2):
                def evz(c, ps, blk=blk):
                    self.act(V(sz[:, c, blk * 512:(blk + 1) * 512], ("sz", c)), ps, AF.Silu)
                self.lin_tm(w_in, Z0 + blk * 512, 512, lambda kt, c: self.hTv(kt, c * 128, 128), evz)

        bc16 = lambda v: V(v.ap.unsqueeze(2).to_broadcast([128, 16, 64]), v.keys)

        def prep_stages(c):
            par = c % 2
            cs = slice(c * 128, (c + 1) * 128)
            XK, BK, SK, MK = ("xs_tok", par), ("B_tok", par), ("sm2", par), ("MT", par)
            DK = ("pre", None) if par == 0 else ("xdt1", None)
            dtc = V(dtr[:, c, :], ("dtr", c))
            xs3 = V(xs_tok[:, par, :].rearrange("p (h q) -> p h q", q=64), XK)

            def p_tr():
                b = self.bank("x")
                for j in range(8):
                    self.tr(self.psb(b, 128, c0=j * 128), V(xbc[:, j, cs], ("xbc", j)), self.identb)
                self.cp(V(xs_tok[:, par, :], XK), self.psb(b, 1024), eng="act")
                b = self.bank("x")
                for j in range(2):
                    self.tr(self.psb(b, 128, c0=j * 128), V(xbc[:, 8 + j, cs], ("xbc", 8 + j)), self.identb)
                self.cp(V(B_tok[:, par, :], BK), self.psb(b, 256), eng="act")

            def p_cs():
                dh = V(dah[:, c, :], ("dah", None))
                dl = V(dal[:, c, :], ("dal", None))
                b = self.bank("x")
                cs_ps = self.psf(b, 16)
                self.mm(cs_ps, UT, dh, start=True, stop=False)
                self.mm(cs_ps, UT, dl, start=False, stop=True)
                tot_ps = self.psf(b, 16, c0=16)
                self.mm(tot_ps, ONE, dh, start=True, stop=False)
                self.mm(tot_ps, ONE, dl, start=False, stop=True)
                cst_ = V(sm2[:, par, 0:16], SK)
                ecs = V(sm2[:, par, 16:32], SK)
                dte = V(sm2[:, par, 32:48], SK)
                cd = V(sm2[:, par, 48:64], SK)
                self.cp(cst_, cs_ps, eng="act")
                self.act(ecs, cs_ps, AF.Exp)
                self.act(cd, tot_ps, AF.Exp)
                self.tt(dte, tot_ps, cst_, ALU.subtract)
                self.act(dte, dte, AF.Exp)

            def p_x():
                dte = V(sm2[:, par, 32:48], SK)
                self.tt(V(xdt[par][:, :, :], DK), xs3, bc16(dtc), ALU.mult)
                self.tt(V(xdd[par][:, :, :], DK), V(xdt[par][:, :, :], DK), bc16(dte), ALU.mult)

            def p_g(g):
                rchv = V(rch[:, :, :], ("rch", None))
                rclv = V(rcl[:, :, :], ("rch", None))
                utb = V(UT.ap.unsqueeze(1).to_broadcast([128, 8, 128]), UT.keys)
                dhb = V(dah[:, c, g * 8:(g + 1) * 8].unsqueeze(2).to_broadcast([128, 8, 128]), ("dah", None))
                dlb = V(dal[:, c, g * 8:(g + 1) * 8].unsqueeze(2).to_broadcast([128, 8, 128]), ("dal", None))
                self.tt(rchv, utb, dhb, ALU.mult)
                self.tt(rclv, utb, dlb, ALU.mult)
                for hf in range(2):
                    b = self.bank("x")
                    sp = self.psf(b, 512)
                    self.mm(sp, SL, V(rch[:, hf * 4:(hf + 1) * 4, :], ("rch", None)), start=True, stop=False)
                    self.mm(sp, SL, V(rcl[:, hf * 4:(hf + 1) * 4, :], ("rch", None)), start=False, stop=True)
                    self.act(V(LT[:, hf * 4:(hf + 1) * 4, :], ("LT", None)), sp, AF.Exp)
                b = self.bank("x")
                cbp = self.psf(b, 128)
                self.mm(cbp, V(xbc[:, 8 + g, cs], ("xbc", 8 + g)), V(xbc[:, 10 + g, cs], ("xbc", 10 + g)))
                cbv = V(cbm[:, g, :], ("cbm", g))
                self.tt(cbv, cbp, UT, ALU.mult)
                self.tt(V(MT[:, par * 16 + g * 8:par * 16 + (g + 1) * 8, :], MK), V(LT[:, :, :], ("LT", None)),
                        V(cbm[:, g, :].unsqueeze(1).to_broadcast([128, 8, 128]), ("cbm", g)), ALU.mult)
            st = [p_tr, p_cs, p_x]
            if not so:
                st += [lambda: p_g(0), lambda: p_g(1)]
            return st

        def fin_stages(c):
            par = c % 2
            cs = slice(c * 128, (c + 1) * 128)
            XK, BK, SK, MK = ("xs_tok", par), ("B_tok", par), ("sm2", par), ("MT", par)
            DK = ("pre", None) if par == 0 else ("xdt1", None)
            xs3 = V(xs_tok[:, par, :].rearrange("p (h q) -> p h q", q=64), XK)
            yv = V(yy[:, :, :], ("acc", None))
            tv = V(tmp[:, :, :], ("tmp", None))
            stt_ = {}

            def f_y():
                by = [self.bank("m"), self.bank("m")]
                for h in range(16):
                    self.mm(V(self.ps[:, by[h // 8], (h % 8) * 64:(h % 8 + 1) * 64], ("ps", by[h // 8])),
                            V(MT[:, par * 16 + h, :], MK), V(xdt[par][:, h, :], DK))
                bo = [self.bank("m"), self.bank("m")]
                for g in range(2):
                    self.mm(self.psf(bo[g], 512), V(xbc[:, 10 + g, cs], ("xbc", 10 + g)),
                            V(self.Sbf[:, g * 512:(g + 1) * 512], ("Sbf", None)))
                for g in range(2):
                    tg = V(tmp[:, g * 8:(g + 1) * 8, :], ("tmp", None))
                    yg = V(yy[:, g * 8:(g + 1) * 8, :], ("acc", None))
                    eg = V(sm2[:, par, 16 + g * 8:16 + (g + 1) * 8].unsqueeze(2).to_broadcast([128, 8, 64]), SK)
                    self.tt(tg, V(self.ps[:, bo[g], :].rearrange("p (h q) -> p h q", q=64), ("ps", bo[g])), eg, ALU.mult)
                    self.tt(yg, V(self.ps[:, by[g], :].rearrange("p (h q) -> p h q", q=64), ("ps", by[g])), tg, ALU.add)

            def f_comb():
                self.tt(tv, xs3, bc16(self.D_bc), ALU.mult)
                self.tt(yv, yv, tv, ALU.add)
                y2 = V(yy[:, :, :].rearrange("p h q -> p (h q)"), ("acc", None))
                self.tt(y2, y2, V(sz[:, c, :], ("sz", c)), ALU.mult)

            def f_norm():
                for g in range(2):
                    self.act(V(self.junk[:, 0:512], ("hn", None)),
                             V(yy[:, g * 8:(g + 1) * 8, :].rearrange("p h q -> p (h q)"), ("acc", None)),
                             AF.Square, accum=V(sm2[:, par, 64 + g:65 + g], SK))
                rs2 = V(sm2[:, par, 64:66], SK)
                self.act(rs2, rs2, AF.Sqrt, bias=self.eps_v, scale=1.0 / 512)
                self.recip(rs2, rs2)
                for g in range(2):
                    self.stt(V(ysn[:, 0, g * 512:(g + 1) * 512], ("ysn", None)),
                             V(yy[:, g * 8:(g + 1) * 8, :].rearrange("p h q -> p (h q)"), ("acc", None)),
                             V(sm2[:, par, 64 + g:65 + g], SK),
                             V(self.ng_bc.ap[:, g * 512:(g + 1) * 512], self.ng_bc.keys), ALU.mult, ALU.mult)

            def f_out():
                b = self.bank("m")
                for j in range(8):
                    self.tr(self.psb(b, 128, c0=j * 128), V(ysn[:, 0, j * 128:(j + 1) * 128], ("ysn", None)), self.identb)
                self.cp(V(self.ymixT[:, 0:8, cs], [("ymixT", j) for j in range(8)]),
                        V(self.psb(b).ap.rearrange("p (a b) -> p a b", b=128), ("ps", b)), eng="act")

            def f_state():
                cd = V(sm2[:, par, 48:64], SK)
                bs = [self.bank("m"), self.bank("m")]
                for g in range(2):
                    self.mm(self.psf(bs[g], 512), V(B_tok[:, par, g * 128:(g + 1) * 128], BK),
                            V(xdd[par][:, g * 8:(g + 1) * 8, :], DK))
                S3 = V(self.Sst[:, :].rearrange("p (h q) -> p h q", q=64), ("Sst", None))
                self.tt(S3, S3, bc16(cd), ALU.mult)
                for g in range(2):
                    sg = V(self.Sst[:, g * 512:(g + 1) * 512], ("Sst", None))
                    self.tt(sg, sg, self.psf(bs[g], 512), ALU.add)
                self.cp(V(self.Sbf[:, :], ("Sbf", None)), V(self.Sst[:, :], ("Sst", None)), eng="act")
            if so:
                return [f_state]
            return [f_y, f_comb, f_norm, f_out, f_state]

        for c in range(NCH + 1):
            pr = prep_stages(c) if c < NCH else []
            fi = fin_stages(c - 1) if c > 0 else []
            for i in range(max(len(pr), len(fi))):
                if i < len(pr):
                    pr[i]()
                if i < len(fi):
                    fi[i]()
        if not so:
            self.debug_dump("ymix_ssd", V(self.ymixT[:, 0:8, :], ("ymixT", None)))
            self.debug_dump("Sst", V(self.Sst[:, :], ("Sst", None)))

    def rwkv(self, so):
        self.begin_phase()
        PG = 4 if so else 2
        o = 0
        tpw, o = self.carve(o, "tpw", [1, T], BF16)
        pab, o = self.carve(o, "pab", [1, T], BF16)
        spg, o = self.carve(o, "spg", [2, T], BF16)
        o_pre = o
        pre, o = self.carve(o, "pre", [2, 1 + T], F32)
        o_dd = o
        dd, o = self.carve(o, "dd", [2, T], F32)
        kraw, o = self.carve(o, "kraw", [4, T], F32)
        vb, o = self.carve(o, "vb", [4, T], BF16)
        rraw, o = self.carve(o, "rraw", [4, T], BF16)
        AR, o = self.carve(o, "AR", [1 if so else 2, PG, T], BF16)
        bt, o = self.carve(o, "bt", [PG, T], BF16)
        kt_, o = self.carve(o, "kt_", [PG, T], BF16)
        bE, o = self.carve(o, "bE", [PG, T], BF16)
        kE, o = self.carve(o, "kE", [PG, T], BF16)
        GC, o = self.carve(o, "GC", [PG, 8], F32)
        if not so:
            bon, o = self.carve(o, "bon", [PG, T], BF16)
            yr, o = self.carve(o, "yr", [PG, T], F32)
        f1, o = self.carve(o, "f1", [1, T], F32)
        f2, o = self.carve(o, "f2", [1, T], F32)
        f3, o = self.carve(o, "f3", [1, T], F32, at=o_pre)
        f4, o = self.carve(o, "f4", [1, T], F32, at=o_pre + 513)
        f5, o = self.carve(o, "f5", [1, T], F32, at=o_dd)
        f6, o = self.carve(o, "f6", [1, T], F32, at=o_dd + 512)
        h1, o = self.carve(o, "h1", [1, T], BF16)
        X1m, o = self.carve4(o, "X1m", [2, PG, 2, 64], BF16)
        X2m, o = self.carve4(o, "X2m", [2, PG, 2, 64], BF16)
        PTf, o = self.carve(o, "PTf", [2, PG, 64], BF16)
        ZP, o = self.carve(o, "ZP", [2, PG, 128], BF16)
        Lb, o = self.carve(o, "Lb", [2, PG, 64], BF16)
        TM, o = self.carve4(o, "TM", [2, PG, 3, 64], BF16)
        Xb, o = self.carve(o, "Xb", [PG, 64], BF16)
        Ub, o = self.carve(o, "Ub", [PG, 64], BF16)
        w_in = self.w_in
        BLK = self.cstv(C_BLK)

        def shift_evac(idx, rows, ps, dst, func=None):
            i = self.rot.get("pre", 0)
            self.rot["pre"] = i + 1
            pb = i % 2
            pv = lambda a, n: V(pre[0:rows, pb, a:a + n], ("pre", pb))
            hal = V(self.shifth[0:rows, idx:idx + 1], ("shifth", idx))
            self.cp(pv(0, 1), hal)
            self.cp(pv(1, T), ps, eng="act")
            self.cp(hal, pv(T, 1))
            dv = V(dd[0:rows, pb, :], ("dd", pb))
            self.tt(dv, pv(0, T), pv(1, T), ALU.subtract)
            mu = V(self.ppt[0:rows, self.PP_MU + idx:self.PP_MU + idx + 1], ("ppt", None))
            if func is None:
                self.stt(dst, dv, mu, pv(1, T), ALU.mult, ALU.add)
            else:
                self.stt(dv, dv, mu, pv(1, T), ALU.mult, ALU.add)
                self.act(dst, dv, func)

        def ev_pwpa(off, width, ps):
            if off == 0:
                shift_evac(24, 96, ps, V(tpw[0:96, 0, :], ("tpw", None)), func=AF.Tanh)
            else:
                shift_evac(25, 96, ps, V(pab[0:96, 0, :], ("pab", None)), func=AF.Copy)
        self.lin_fm(w_in, PW0, [(0, 96), (96, 96)], ev_pwpa)
        if self.halo_all:
            def ev_pg(off, width, ps):
                j = off // 128
                shift_evac(26 + j, 128, ps, V(spg[:, j, :], ("spg", j)), func=AF.Sigmoid)
            self.lin_fm(w_in, PG0, [(0, 128), (128, 128)], ev_pg)

        HH = [(0, 64), (64, 128)]
        bcq = lambda blk, n: V(self.cstb[:, blk * 128:blk * 128 + n].unsqueeze(1).to_broadcast([128, PG, n]), ("cstb", None))
        M1, M1s, SL64, ID64 = bcq(C_M1, 128), bcq(C_M1, 64), bcq(C_SL64, 64), bcq(C_ID64, 64)
        v3k = ("pre", None)
        v5k = ("dd", None)
        for gq in range(2):
            def ev_k(off, width, ps):
                fl = off // 128
                shift_evac(8 + gq * 4 + fl, 128, ps, V(kraw[:, fl, :], ("kraw", fl)))

            def ev_v(off, width, ps):
                fl = off // 128
                shift_evac(16 + gq * 4 + fl, 128, ps, V(vb[:, fl, :], ("vb", fl)))

            def ev_r(off, width, ps):
                fl = off // 128
                shift_evac(gq * 4 + fl, 128, ps, V(rraw[:, fl, :], ("rraw", fl)))
            t4 = [(i * 128, 128) for i in range(4)]
            self.lin_fm(w_in, K0 + gq * 512, t4, ev_k)
            self.lin_fm(w_in, V0 + gq * 512, t4, ev_v)
            if self.halo_all:
                self.lin_fm(w_in, R0 + gq * 512, t4, ev_r)
            if so:
                self.convert_some()
            for sg in range(4 // PG):
                for q in range(PG):
                    fl = sg * PG + q
                    f = gq * 4 + fl
                    v1, v2 = V(f1[:, 0, :], ("f1", None)), V(f2[:, 0, :], ("f2", None))
                    v3, v4 = V(f3[:, 0, :], v3k), V(f4[:, 0, :], v3k)
                    v5, v6 = V(f5[:, 0, :], v5k), V(f6[:, 0, :], v5k)
                    hv = V(h1[:, 0, :], ("h1", None))
                    kr = V(kraw[:, fl, :], ("kraw", fl))
                    b = self.bank("x")
                    pw_ps = self.psf(b, T)
                    self.mm(pw_ps, V(self.lorab[0:96, 0, f * 128:(f + 1) * 128], ("lorab", 0)), V(tpw[0:96, 0, :], ("tpw", None)))
                    self.act(v1, pw_ps, AF.Sigmoid, bias=self.ppv(self.PP_W0 + f))
                    rstm = V(self.cstb[:, C_RST * 128:C_RST * 128 + T], ("cstb", None))
                    o1, a1, b1 = v2.ap, rstm.ap, v1.ap
                    self.S.add("dve", lambda e, o1=o1, a1=a1, b1=b1: e.tensor_tensor_scan(o1, a1, b1, 0.0, ALU.mult, ALU.add),
                               r=[rstm, v1], w=[v2])
                    b = self.bank("x")
                    pa_ps = self.psf(b, T)
                    self.mm(pa_ps, V(self.lorab[0:96, 1, f * 128:(f + 1) * 128], ("lorab", 1)), V(pab[0:96, 0, :], ("pab", None)))
                    self.act(v3, pa_ps, AF.Sigmoid, bias=self.ppv(self.PP_A0 + f))
                    self.act(v4, kr, AF.Copy, scale=self.ppv(self.PP_KK + f))
                    self.act(hv, kr, AF.Square, scale=self.ppv(self.PP_KK + f))
                    b = self.bank("x")
                    ss_ps = self.psf(b, T)
                    self.mm(ss_ps, BLK, hv)
                    self.act(v5, ss_ps, AF.Sqrt)
                    self.ts(v5, v5, 1e-12, ALU.max)
                    self.recip(v5, v5)
                    self.tt(v4, v4, v5, ALU.mult)
                    self.act(v5, v3, AF.Identity, bias=self.ppv(self.PP_OMKA + f), scale=self.ppv(self.PP_KA + f))
                    self.tt(v5, v5, kr, ALU.mult)
                    if not so:
                        rr = V(rraw[:, fl, :], ("rraw", fl))
                        self.tt(v6, rr, v5, ALU.mult)
                        self.act(hv, v6, AF.Copy, scale=self.ppv(self.PP_RK + f))
                        b = self.bank("x")
                        bo_ps = self.psf(b, T)
                        self.mm(bo_ps, BLK, hv)
                        self.tt(V(bon[:, q, :], ("bon", q)), bo_ps, V(vb[:, fl, :], ("vb", fl)), ALU.mult)
                    self.tt(v3, v3, v4, ALU.mult)
                    self.act(v6, v2, AF.Exp, scale=-ESQ)
                    self.cp(V(GC[:, q, :], ("GC", q)), V(f6[:, 0, 63::64], v5k))
                    if not so:
                        self.tt(V(AR[:, 1, q, :], ("AR", q)), V(rraw[:, fl, :], ("rraw", fl)), v6, ALU.mult)
                    self.tt(v1, v2, v1, ALU.subtract)
                    self.act(v1, v1, AF.Exp, scale=-ESQ)
                    self.stt(V(AR[:, 0, q, :], ("AR", q)), v4, -1.0, v1, ALU.mult, ALU.mult)
                    self.act(v1, v2, AF.Exp, scale=ESQ)
                    self.tt(V(bt[:, q, :], ("bt", q)), v3, v1, ALU.mult)
                    self.tt(V(kt_[:, q, :], ("kt_", q)), v5, v1, ALU.mult)
                    v1_3 = V(f1[:, 0, :].rearrange("p (j q) -> p j q", q=64), ("f1", None))
                    self.tt(v1_3, v1_3, V(GC[:, q, :].unsqueeze(2).to_broadcast([128, 8, 64]), ("GC", q)), ALU.mult)
                    self.tt(V(bE[:, q, :], ("bE", q)), v3, v1, ALU.mult)
                    self.tt(V(kE[:, q, :], ("kE", q)), v5, v1, ALU.mult)
                nar = 1 if so else 2
                ps3 = lambda bk, w: self.ps[:, bk, 0:PG * w].rearrange("p (q c) -> p q c", c=w)
                f0 = gq * 4 + sg * PG

                def inv_stages(j):
                    par = j % 2
                    c64 = slice(j * 64, (j + 1) * 64)
                    X1k, X2k, TMk, PTk = ("X1m", par), ("X2m", par), ("TM", par), ("PTf", par)

                    def s_tr():
                        bt_ = self.bank("x")
                        for q in range(PG):
                            fl = sg * PG + q
                            for (p0, p1) in HH:
                                idn = V(self.cstb[p0:p1, C_ID * 128 + p0:C_ID * 128 + p1], ("cstb", None))
                                for n_, srcv in enumerate([V(bE[p0:p1, q, c64], ("bE", q)), V(kE[p0:p1, q, c64], ("kE", q)),
                                                           V(vb[p0:p1, fl, c64], ("vb", fl))]):
                                    self.tr(V(self.ps[p0:p1, bt_, :].bitcast(BF16)[:, (q * 3 + n_) * 64:(q * 3 + n_ + 1) * 64], ("ps", bt_)),
                                            srcv, idn)
                        self.cp(V(TM[:, par, :, :, :].rearrange("p q a b -> p (q a b)"), TMk), self.psb(bt_, PG * 192), eng="act")

                    def s0():
                        b1, b2, b3 = self.bank("x"), self.bank("x"), self.bank("x")
                        for q in range(PG):
                            for (p0, p1) in HH:
                                rhs_ar = V(AR[p0:p1, 0:nar, q, c64], ("AR", q))
                                self.mm(V(self.ps[p0:p1, b1, q * 128:q * 128 + nar * 64].rearrange("p (a b) -> p a b", b=64), ("ps", b1)),
                                        V(bt[p0:p1, q, c64], ("bt", q)), rhs_ar)
                                self.mm(V(self.ps[p0:p1, b2, q * 128:q * 128 + nar * 64].rearrange("p (a b) -> p a b", b=64), ("ps", b2)),
                                        V(kt_[p0:p1, q, c64], ("kt_", q)), rhs_ar)
                                self.mm(V(self.ps[p0:p1, b3, q * 64:(q + 1) * 64], ("ps", b3)),
                                        V(AR[p0:p1, 0, q, c64], ("AR", q)), V(bt[p0:p1, q, c64], ("bt", q)))
                        if so:
                            self.tt(V(X1m[:, par, :, 0, :], X1k), V(ps3(b1, 128)[:, :, 0:64], ("ps", b1)), M1s, ALU.mult)
                            self.tt(V(X2m[:, par, :, 0, :], X2k), V(ps3(b2, 128)[:, :, 0:64], ("ps", b2)), M1s, ALU.mult)
                        else:
                            self.tt(V(X1m[:, par, :, :, :].rearrange("p q a b -> p q (a b)"), X1k), V(ps3(b1, 128), ("ps", b1)), M1, ALU.mult)
                            self.tt(V(X2m[:, par, :, :, :].rearrange("p q a b -> p q (a b)"), X2k), V(ps3(b2, 128), ("ps", b2)), M1, ALU.mult)
                        self.tt(V(Lb[:, 0, :, :], ("Lb", 0)), V(ps3(b3, 64), ("ps", b3)), SL64, ALU.mult)
                        self.cp(V(ZP[:, 0, :, 0:64], ("ZP", 0)), V(X1m[:, par, :, 0, :], X1k))
                        self.tt(V(ZP[:, 0, :, 64:128], ("ZP", 0)), V(X1m[:, par, :, 0, :], X1k), ID64, ALU.add)

                    def sk(k):
                        cur, nxt = (k - 1) % 2, k % 2
                        ba, bb = self.bank("x"), self.bank("x")
                        for q in range(PG):
                            for (p0, p1) in HH:
                                if k == 1:
                                    rz = V(ZP[p0:p1, cur, q, 0:64], ("ZP", cur))
                                    oz = V(self.ps[p0:p1, ba, q * 128:q * 128 + 64], ("ps", ba))
                                elif k == 5:
                                    rz = V(ZP[p0:p1, cur, q, 64:128], ("ZP", cur))
                                    oz = V(self.ps[p0:p1, ba, q * 128 + 64:q * 128 + 128], ("ps", ba))
                                else:
                                    rz = V(ZP[p0:p1, cur, q, :], ("ZP", cur))
                                    oz = V(self.ps[p0:p1, ba, q * 128:(q + 1) * 128], ("ps", ba))
                                self.mm(oz, V(Lb[p0:p1, cur, q, :], ("Lb", cur)), rz)
                                self.mm(V(self.ps[p0:p1, bb, q * 64:(q + 1) * 64], ("ps", bb)),
                                        V(ZP[p0:p1, cur, q, 0:64], ("ZP", cur)), V(Lb[p0:p1, cur, q, :], ("Lb", cur)))
                        pa3 = ps3(ba, 128)
                        if k < 5:
                            self.cp(V(ZP[:, nxt, :, 0:64], ("ZP", nxt)), V(pa3[:, :, 0:64], ("ps", ba)), eng="act")
                        if k == 1:
                            self.cp(V(ZP[:, nxt, :, 64:128], ("ZP", nxt)), V(ZP[:, cur, :, 64:128], ("ZP", cur)))
                        else:
                            self.tt(V(ZP[:, nxt, :, 64:128], ("ZP", nxt)), V(pa3[:, :, 64:128], ("ps", ba)),
                                    V(ZP[:, cur, :, 64:128], ("ZP", cur)), ALU.add)
                        self.cp(V(Lb[:, nxt, :, :], ("Lb", nxt)), V(ps3(bb, 64), ("ps", bb)), eng="act")

                    def s_fin():
                        cur = 1
                        ba = self.bank("x")
                        for q in range(PG):
                            for (p0, p1) in HH:
                                self.mm(V(self.ps[p0:p1, ba, q * 64:(q + 1) * 64], ("ps", ba)),
                                        V(Lb[p0:p1, cur, q, :], ("Lb", cur)), V(ZP[p0:p1, cur, q, 64:128], ("ZP", cur)))
                        self.tt(V(PTf[:, par, :, :], PTk), V(ps3(ba, 64), ("ps", ba)),
                                V(ZP[:, cur, :, 64:128], ("ZP", cur)), ALU.add)
                    return [s_tr, s0] + [(lambda k=k: sk(k)) for k in range(1, 6)] + [s_fin]

                def chain_stages(j):
                    par = j % 2
                    c64 = slice(j * 64, (j + 1) * 64)
                    X1k, X2k, TMk, PTk = ("X1m", par), ("X2m", par), ("TM", par), ("PTf", par)

                    def cX():
                        bx = self.bank("m")
                        for q in range(PG):
                            fq = f0 + q
                            for (p0, p1) in HH:
                                ox = V(self.ps[p0:p1, bx, q * 64:(q + 1) * 64], ("ps", bx))
                                self.mm(ox, V(AR[p0:p1, 0, q, c64], ("AR", q)), V(self.Tbf[p0:p1, fq, :], ("Tbf", fq)), start=True, stop=False)
                                self.mm(ox, V(X2m[p0:p1, par, q, 0, :], X2k), V(TM[p0:p1, par, q, 2, :], TMk), start=False, stop=True)
                        self.cp(V(Xb[:, :, :], ("Xb", None)), V(ps3(bx, 64), ("ps", bx)), eng="act")

                    def cU():
                        bu = self.bank("m")
                        for q in range(PG):
                            for (p0, p1) in HH:
                                self.mm(V(self.ps[p0:p1, bu, q * 64:(q + 1) * 64], ("ps", bu)), V(PTf[p0:p1, par, q, :], PTk),
                                        V(Xb[p0:p1, q, :], ("Xb", None)))
                        self.cp(V(Ub[:, :, :], ("Ub", None)), V(ps3(bu, 64), ("ps", bu)), eng="act")

                    def cY():
                        if so:
                            return
                        by = self.bank("m")
                        for q in range(PG):
                            fq = f0 + q
                            for (p0, p1) in HH:
                                oy = V(self.ps[p0:p1, by, q * 64:(q + 1) * 64], ("ps", by))
                                self.mm(oy, V(self.Tbf[p0:p1, fq, :], ("Tbf", fq)), V(AR[p0:p1, 1, q, c64], ("AR", q)), start=True, stop=False)
                                self.mm(oy, V(Ub[p0:p1, q, :], ("Ub", None)), V(X1m[p0:p1, par, q, 1, :], X1k), start=False, stop=False)
                                self.mm(oy, V(TM[p0:p1, par, q, 2, :], TMk), V(X2m[p0:p1, par, q, 1, :], X2k), start=False, stop=True)
                        self.cp(V(yr[:, :, c64], ("yr", None)), V(ps3(by, 64), ("ps", by)), eng="act")

                    def cT():
                        bn = self.bank("m")
                        for q in range(PG):
                            for (p0, p1) in HH:
                                on = V(self.ps[p0:p1, bn, q * 64:(q + 1) * 64], ("ps", bn))
                                self.mm(on, V(TM[p0:p1, par, q, 0, :], TMk), V(Ub[p0:p1, q, :], ("Ub", None)), start=True, stop=False)
                                self.mm(on, V(TM[p0:p1, par, q, 1, :], TMk), V(TM[p0:p1, par, q, 2, :], TMk), start=False, stop=True)
                        Tg = V(self.Tst[:, f0:f0 + PG, :], [("Tst", f0 + q) for q in range(PG)])
                        self.tt(Tg, Tg, V(GC[:, :, j:j + 1].to_broadcast([128, PG, 64]), ("GC", None)), ALU.mult)
                        self.tt(Tg, Tg, V(ps3(bn, 64), ("ps", bn)), ALU.add)
                        self.cp(V(self.Tbf[:, f0:f0 + PG, :], [("Tbf", f0 + q) for q in range(PG)]), Tg, eng="act")
                    return [cX, cU, cY, cT]

                NJ = T // 64
                for j in range(NJ + 1):
                    inv = inv_stages(j) if j < NJ else []
                    ch = chain_stages(j - 1) if j > 0 else []
                    ci = 0
                    for s_i, st_fn in enumerate(inv):
                        st_fn()
                        if s_i % 2 == 1 and ci < len(ch):
                            ch[ci]()
                            ci += 1
                    while ci < len(ch):
                        ch[ci]()
                        ci += 1
                if not so:
                    for q in range(PG):
                        f = gq * 4 + sg * PG + q
                        v1 = V(f1[:, 0, :], ("f1", None))
                        v2 = V(f2[:, 0, :], ("f2", None))
                        hv = V(h1[:, 0, :], ("h1", None))
                        yv = V(yr[:, q, :], ("yr", None))
                        self.cp(hv, yv)
                        b = self.bank("x")
                        m_ps = self.psf(b, T)
                        self.mm(m_ps, BLK, hv)
                        self.stt(v1, m_ps, -1.0 / 64, yv, ALU.mult, ALU.add)
                        self.act(hv, v1, AF.Square)
                        b = self.bank("x")
                        v_ps = self.psf(b, T)
                        self.mm(v_ps, BLK, hv)
                        self.act(v2, v_ps, AF.Sqrt, bias=self.ppv(254), scale=1.0 / 64)
                        self.recip(v2, v2)
                        self.tt(v1, v1, v2, ALU.mult)
                        self.ts(v1, v1, self.ppv(self.PP_LNW + f), ALU.mult, self.ppv(self.PP_LNB + f), ALU.add)
                        self.tt(v1, v1, V(bon[:, q, :], ("bon", q)), ALU.add)
                        b = self.bank("x")
                        g_ps = self.psf(b, T)
                        for k2 in range(2):
                            self.mm(g_ps, V(self.lorab[:, 2 + k2, f * 128:(f + 1) * 128], ("lorab", 2)), V(spg[:, k2, :], ("spg", k2)),
                                    start=(k2 == 0), stop=(k2 == 1))
                        self.tt(V(self.ymixT[:, 8 + f, :], ("ymixT", 8 + f)), g_ps, v1, ALU.mult)
        if not so:
            self.debug_dump("ymix_rwkv", V(self.ymixT[:, 8:16, :], ("ymixT", None)))
            self.debug_dump("Tst", V(self.Tst[:, :, :], ("Tst", None)))

    def resid_tm(self, w2d, lhs, k0=0):
        for blk in range(4):
            def ev(c, ps, blk=blk):
                xv = V(self.xres[:, c, blk * 512:(blk + 1) * 512], ("xres", c))
                self.tt(xv, ps, xv, ALU.add)
            self.lin_tm(w2d, blk * 512, 512, lhs, ev, k0=k0)

    def out_proj(self):
        self.resid_tm(self.w_out, lambda kt, c: V(self.ymixT[:, kt, c * 128:(c + 1) * 128], ("ymixT", kt)))

    def xattn(self):
        self.rmsnorm_hT(self.xc, self.ppv(self.PP_G2, 16))
        self.begin_phase()
        o = 0
        qT, o = self.carve(o, "qT", [KT, T], BF16)
        PTt, o = self.carve(o, "PTt", [4, T], BF16)
        pex, o = self.carve(o, "pex", [NCH, 256], F32)
        pn, o = self.carve(o, "pn", [NCH, 256], BF16)
        sm3, o = self.carve(o, "sm3", [NCH, 8], F32)
        oT = self.ymixT
        for blk in range(4):
            def evq(off, width, ps, blk=blk):
                self.cp(V(qT[:, blk * 4 + off // 128, :], ("qT", blk * 4 + off // 128)), ps, eng="act")
            self.lin_fm(self.wq, blk * 512, [(i * 128, 128) for i in range(4)], evq)
        sc = 512 ** -0.5
        for hh in range(4):
            hp = hh % 2
            sps = {}
            for c in range(NCH):
                bk = self.bank("x")
                sp = self.psf(bk, 256)
                sps[c] = sp
                for d in range(4):
                    self.mm(sp, V(qT[:, hh * 4 + d, c * 128:(c + 1) * 128], ("qT", hh * 4 + d)),
                            V(self.kT[:, hh * 4 + d, :], ("kT", None)), start=(d == 0), stop=(d == 3))
            for c in range(NCH):
                mx = V(sm3[:, c, 0:1], ("sm3", c))
                o1, i1 = mx.ap, sps[c].ap
                self.S.add("dve", lambda e, o1=o1, i1=i1: e.reduce_max(o1, i1, AX.X), r=[sps[c]], w=[mx])
                self.ts(V(sm3[:, c, 1:2], ("sm3", c)), mx, -sc, ALU.mult)
            for c in range(NCH):
                self.act(V(pex[:, c, :], ("pex", c)), sps[c], AF.Exp, bias=V(sm3[:, c, 1:2], ("sm3", c)), scale=sc,
                         accum=V(sm3[:, c, 2:3], ("sm3", c)))
            for c in range(NCH):
                rs = V(sm3[:, c, 2:3], ("sm3", c))
                self.recip(rs, rs)
                self.ts(V(pn[:, c, :], ("pn", c)), V(pex[:, c, :], ("pex", c)), rs, ALU.mult)
            for c in range(NCH):
                bk = self.bank("m")
                for mt in range(2):
                    self.tr(self.psb(bk, 128, c0=mt * 128), V(pn[:, c, mt * 128:(mt + 1) * 128], ("pn", c)), self.identb)
                self.cp(V(PTt[:, hp * 2:hp * 2 + 2, c * 128:(c + 1) * 128], ("PTt", (hp, c))),
                        V(self.psb(bk, 256).ap.rearrange("p (a b) -> p a b", b=128), ("ps", bk)), eng="act")
            for d in range(4):
                bk = self.bank("m")
                op_ = self.psf(bk, T)
                for mt in range(2):
                    self.mm(op_, V(self.Vm[:, mt, hh * 512 + d * 128:hh * 512 + (d + 1) * 128], ("Vm", None)),
                            V(PTt[:, hp * 2 + mt, :], [("PTt", (hp, c)) for c in range(NCH)]), start=(mt == 0), stop=(mt == 1))
                self.cp(V(oT[:, hh * 4 + d, :], ("ymixT", hh * 4 + d)), op_, eng="act")
        self.resid_tm(self.wo, lambda kt, c: V(oT[:, kt, c * 128:(c + 1) * 128], ("ymixT", kt)))

    def ffn(self):
        self.rmsnorm_hT(self.xc, self.ppv(self.PP_G3, 16))
        self.begin_phase()
        o = 0
        g1T, o = self.carve(o, "g1T", [KT, T], BF16)
        rl, o = self.carve(o, "rl", [2, T], F32)
        for qt in range(4):
            for blk in range(4):
                def ev1(off, width, ps, blk=blk):
                    i = self.rot.get("rl", 0)
                    self.rot["rl"] = i + 1
                    r = V(rl[:, i % 2, :], ("rl", i % 2))
                    self.act(r, ps, AF.Relu)
                    kt = blk * 4 + off // 128
                    self.tt(V(g1T[:, kt, :], ("g1T", kt)), r, r, ALU.mult)
                self.lin_fm(self.w1, qt * 2048 + blk * 512, [(i * 128, 128) for i in range(4)], ev1)
            self.resid_tm(self.w2, lambda kt, c: V(g1T[:, kt, c * 128:(c + 1) * 128], ("g1T", kt)), k0=qt * 2048)

    def convert_some(self):
        for _ in range(self.conv_per_call):
            if self.conv_list:
                self.load_w(*self.conv_list.pop(0))

    def final(self, yout, i, nxt=None):
        self.begin_phase()
        o = 0
        ob, o = self.carve(o, "ob", [2, D], F32)
        gfb, o = self.carve(o, "gfb", [1, D], F32)
        self.gf_bc = V(gfb[:, 0, :], ("gfb", None))
        self.dma(self.gf_bc, self.bc_dram.partition_broadcast(128)[:, 0, 48 + 1024:48 + 1024 + 2048])
        for c in range(NCH):
            x = self.xc(c)
            junk = V(self.junk[:, :], ("hn", None))
            ss = V(self.sm[:, 0:1], ("sm", 0))
            self.act(junk, x, AF.Square, accum=ss)
            sq = V(self.sm[:, 1:2], ("sm", 1))
            self.act(sq, ss, AF.Sqrt, bias=self.eps_v, scale=1.0 / D)
            rstd = V(self.sm[:, 2:3], ("sm", 2))
            self.recip(rstd, sq)
            ov = V(ob[:, c % 2, :], ("ob", c % 2))
            self.stt(ov, x, rstd, self.gf_bc, ALU.mult, ALU.mult)
            self.dma(yout[i * T + c * 128:i * T + (c + 1) * 128, :], ov)
            if nxt is not None:
                self.dma(V(self.xres[:, c, :], ("xres", c)), nxt[0][nxt[1] * T + c * 128:nxt[1] * T + (c + 1) * 128, :])


def host_tables(inp):
    L = 0
    pp = np.zeros((128, 256), np.float32)
    col = lambda v, n: np.asarray(v, np.float32).reshape(n, 128).T
    pp[:, 0:16] = col(inp["norm_mix_g"][L], 16)
    pp[:, 16:32] = col(inp["norm_x_g"][L], 16)
    pp[:, 32:48] = col(inp["norm_ffn_g"][L], 16)
    pp[:, 48:64] = col(inp["norm_mem_g"][L], 16)
    cw = np.asarray(inp["ssd_conv_w"][L], np.float32)[:, 0, :]
    pp[:, 64:112] = cw.T.reshape(12, 128, 4).transpose(1, 0, 2).reshape(128, 48)
    pp[:, 112:124] = col(inp["ssd_conv_b"][L], 12)
    mu = np.asarray(inp["rwkv_mu"][L], np.float32)
    pp[:, 124:148] = col(mu[0:3072], 24)
    pp[0:96, 148] = mu[3072:3168]
    pp[0:96, 149] = mu[3168:3264]
    pp[:, 150:152] = col(mu[3264:3520], 2)
    pp[:, 152:160] = col(inp["rwkv_w0"][L], 8)
    pp[:, 160:168] = col(inp["rwkv_a0"][L], 8)
    pp[:, 168:176] = col(inp["rwkv_k_k"][L], 8)
    ka = np.asarray(inp["rwkv_k_a"][L], np.float32)
    pp[:, 176:184] = col(ka, 8)
    pp[:, 192:200] = col(np.asarray(inp["rwkv_r_k"][L], np.float32).reshape(-1), 8)
    pp[:, 200:208] = col(inp["rwkv_ln_w"][L], 8)
    pp[:, 208:216] = col(inp["rwkv_ln_b"][L], 8)
    pp[:, 253] = 1.0
    pp[:, 254] = LN_EPS
    pp[:, 255] = EPS
    bc = np.concatenate([
        np.asarray(inp["ssd_a_log"][L], np.float32), np.asarray(inp["ssd_dt_bias"][L], np.float32),
        np.asarray(inp["ssd_d"][L], np.float32), np.asarray(inp["ssd_norm_g"][L], np.float32),
        np.asarray(inp["final_norm_g"], np.float32)])[None, :]
    lora = np.concatenate([np.asarray(inp["rwkv_w2"][L], np.float32), np.asarray(inp["rwkv_a2"][L], np.float32),
                           np.asarray(inp["rwkv_g2"][L], np.float32)], axis=0)
    return pp, np.ascontiguousarray(bc), np.ascontiguousarray(lora)


def core_inputs(inp, xo, xp, mem_b, flagv, tabs, cst):
    pp, bc, lora = tabs
    f32 = lambda a: np.ascontiguousarray(np.asarray(a, np.float32))
    return {
        "xo": f32(xo), "xp": f32(xp), "flag": np.full((128, 1), flagv, np.float32), "mem": f32(mem_b), "cst": cst,
        "w_in": f32(inp["w_in"][0]), "w_out": f32(inp["w_out"][0]), "wq": f32(inp["xattn_wq"][0]),
        "wk": f32(inp["xattn_wk"][0]), "wv": f32(inp["xattn_wv"][0]), "wo": f32(inp["xattn_wo"][0]),
        "w1": f32(inp["ffn_w1"][0]), "w2": f32(inp["ffn_w2"][0]), "pp": pp, "bc": bc, "lora": lora,
    }


def kernel(**inp):
    x = np.asarray(inp["x"], np.float32)
    mem = np.asarray(inp["mem"], np.float32)
    Bn, Sq, _ = x.shape
    half = Sq // 2
    nt = half // T
    bld = Builder(nt, nt)
    nc = bld.build()
    tabs = host_tables(inp)
    cst = make_consts()
    in_maps = []
    for core in range(8):
        b, h = core // 2, core % 2
        in_maps.append(core_inputs(inp, x[b, h * half:(h + 1) * half], x[b, 0:half], mem[b], float(h), tabs, cst))
    res = run_bass_kernel_spmd(nc, in_maps, core_ids=list(range(8)))
    out = np.zeros((Bn, Sq, D), np.float32)
    for core in range(8):
        b, h = core // 2, core % 2
        out[b, h * half:(h + 1) * half] = np.asarray(res.results[core]["y"], np.float32)
    return out
```
